# Optimizing a Trainium2 kernel written in Bass

```python
import functools
import jax, jax.numpy as jnp
from jax import lax
import numpy as np

D_MODEL = 2048
BATCH = 4
SEQ = 2048
DEPTH = 1
DEC_BATCH = 128
DEC_SEQ = 8
PAST_LEN = 16384
PAGE_SIZE = 128

ATTN_WIDTH = D_MODEL // 2
CONV_CH = D_MODEL - ATTN_WIDTH
HEAD_DIM = 64
N_HEADS = ATTN_WIDTH // HEAD_DIM
N_KV_HEADS = max(1, N_HEADS // 4)
GROUP = N_HEADS // N_KV_HEADS
KV_WIDTH = N_KV_HEADS * HEAD_DIM
WINDOW = 128
BLOCK = 128
CONV_K = 3
D_FF = ((8 * D_MODEL // 3 + 255) // 256) * 256
IN_WIDTH = ATTN_WIDTH + 2 * KV_WIDTH + 3 * CONV_CH
SPLITS = [ATTN_WIDTH, ATTN_WIDTH + KV_WIDTH, ATTN_WIDTH + 2 * KV_WIDTH,
          ATTN_WIDTH + 2 * KV_WIDTH + CONV_CH, ATTN_WIDTH + 2 * KV_WIDTH + 2 * CONV_CH]
EPS = 1e-6
NEG = -1e30

kernel_name = "hymba_swa_sink_shortconv_convffn_step"


def _rmsnorm(x, g):
    xf = x.astype(jnp.float32)
    y = xf * lax.rsqrt(jnp.mean(xf * xf, axis=-1, keepdims=True) + EPS) * g.astype(jnp.float32)
    return y.astype(x.dtype)


def _alibi_slopes():
    h = jnp.arange(1, N_HEADS + 1, dtype=jnp.float32)
    return jnp.exp2(-8.0 * h / N_HEADS).reshape(N_KV_HEADS, GROUP)


def _sink_attend(q, k, v, dist, valid, sinks):
    s = jnp.einsum('...qkgd,...skd->...kgqs', q, k,
                   preferred_element_type=jnp.float32) * (HEAD_DIM ** -0.5)
    slopes = _alibi_slopes()[:, :, None, None]
    s = jnp.where(valid, s - slopes * dist, NEG)
    sink = sinks.astype(jnp.float32).reshape(N_KV_HEADS, GROUP)[:, :, None]
    m = jnp.maximum(jnp.max(s, axis=-1), sink)
    p = jnp.exp(s - m[..., None])
    denom = jnp.sum(p, axis=-1) + jnp.exp(sink - m)
    w = (p / denom[..., None]).astype(v.dtype)
    return jnp.einsum('...kgqs,...skd->...qkgd', w, v)


def _attend_prompt(q, k, v, sinks):
    b, t = q.shape[:2]
    nb = t // BLOCK
    qb = q.reshape(b, nb, BLOCK, N_KV_HEADS, GROUP, HEAD_DIM)

    def with_prev(a):
        a = a.reshape(b, nb, BLOCK, N_KV_HEADS, HEAD_DIM)
        prev = jnp.concatenate([jnp.zeros_like(a[:, :1]), a[:, :-1]], axis=1)
        return jnp.concatenate([prev, a], axis=2)

    qi = jnp.arange(BLOCK)[:, None]
    kj = jnp.arange(2 * BLOCK)[None, :]
    dist = qi - kj + BLOCK
    key_pos = jnp.arange(nb)[:, None, None] * BLOCK + kj[None] - BLOCK
    valid = (dist >= 0) & (dist <= WINDOW) & (key_pos >= 0)
    out = _sink_attend(qb, with_prev(k), with_prev(v), dist.astype(jnp.float32),
                       valid[:, None, None], sinks)
    return out.reshape(b, t, ATTN_WIDTH)


def _attend_sample(q, k, v, sinks, k_buf, v_buf):
    b, t = q.shape[:2]
    kk = jnp.concatenate([k_buf.astype(k.dtype), k], axis=1)
    vv = jnp.concatenate([v_buf.astype(v.dtype), v], axis=1)
    qi = jnp.arange(t)[:, None]
    kj = jnp.arange(WINDOW + t)[None, :]
    dist = qi + WINDOW - kj
    valid = (dist >= 0) & (dist <= WINDOW)
    out = _sink_attend(q.reshape(b, t, N_KV_HEADS, GROUP, HEAD_DIM), kk, vv,
                       dist.astype(jnp.float32), valid, sinks)
    return out.reshape(b, t, ATTN_WIDTH)


def _causal_dwconv(u, prev, w):
    t = u.shape[1]
    ext = jnp.concatenate([prev.astype(u.dtype), u], axis=1)
    out = w[0] * ext[:, 0:t]
    for j in range(1, CONV_K):
        out = out + w[j] * ext[:, j:j + t]
    return out, ext[:, t:]


def _layer(x, attend_fn, conv_prev, ffn_prev, g_attn_norm, w_in, attn_sinks, conv_w,
           g_out_attn, g_out_conv, w_out, g_ffn_norm, w_gate, w_up, ffn_conv_w, ffn_conv_b, w_down):
    b, t, _ = x.shape
    h = _rmsnorm(x, g_attn_norm)
    q, k, v, gate_b, gate_c, u_in = jnp.split(h @ w_in, SPLITS, axis=-1)
    k = k.reshape(b, t, N_KV_HEADS, HEAD_DIM)
    v = v.reshape(b, t, N_KV_HEADS, HEAD_DIM)
    attn = attend_fn(q, k, v, attn_sinks)
    conv_out, conv_state = _causal_dwconv(gate_c * u_in, conv_prev, conv_w)
    sconv = gate_b * conv_out
    mixed = jnp.concatenate([_rmsnorm(attn, g_out_attn), _rmsnorm(sconv, g_out_conv)], axis=-1)
    x = x + mixed @ w_out
    h = _rmsnorm(x, g_ffn_norm)
    a, ffn_state = _causal_dwconv(h @ w_gate, ffn_prev, ffn_conv_w)
    x = x + (jax.nn.silu(a + ffn_conv_b) * (h @ w_up)) @ w_down
    return x, k, v, conv_state, ffn_state


def setup_inputs(seed: int = 0) -> dict:
    key = jax.random.key(seed)
    ks = jax.random.split(key, 24)
    f32 = jnp.float32
    nrm = lambda k, shape, s=1.0: (jax.random.normal(k, shape, f32) * s).astype(f32)
    gain = lambda k, shape: 1.0 + 0.02 * jax.random.normal(k, shape, f32)
    return {
        "x_prompt": nrm(ks[0], (BATCH, SEQ, D_MODEL)),
        "x_sample": nrm(ks[1], (DEC_BATCH, DEC_SEQ, D_MODEL)),
        "cache_k_window": nrm(ks[2], (DEPTH, DEC_BATCH, WINDOW, N_KV_HEADS, HEAD_DIM)),
        "cache_v_window": nrm(ks[3], (DEPTH, DEC_BATCH, WINDOW, N_KV_HEADS, HEAD_DIM)),
        "state_conv": nrm(ks[4], (DEPTH, DEC_BATCH, CONV_K - 1, CONV_CH)),
        "state_ffn_conv": nrm(ks[5], (DEPTH, DEC_BATCH, CONV_K - 1, D_FF)),
        "g_attn_norm": gain(ks[6], (DEPTH, D_MODEL)),
        "w_in": nrm(ks[7], (DEPTH, D_MODEL, IN_WIDTH), D_MODEL ** -0.5),
        "attn_sinks": nrm(ks[8], (DEPTH, N_HEADS), 0.5),
        "conv_w": nrm(ks[9], (DEPTH, CONV_K, CONV_CH), CONV_K ** -0.5),
        "g_out_attn": gain(ks[10], (DEPTH, ATTN_WIDTH)),
        "g_out_conv": gain(ks[11], (DEPTH, CONV_CH)),
        "w_out": nrm(ks[12], (DEPTH, D_MODEL, D_MODEL), D_MODEL ** -0.5),
        "g_ffn_norm": gain(ks[13], (DEPTH, D_MODEL)),
        "w_gate": nrm(ks[14], (DEPTH, D_MODEL, D_FF), D_MODEL ** -0.5),
        "w_up": nrm(ks[15], (DEPTH, D_MODEL, D_FF), D_MODEL ** -0.5),
        "ffn_conv_w": nrm(ks[16], (DEPTH, CONV_K, D_FF), CONV_K ** -0.5),
        "ffn_conv_b": nrm(ks[17], (DEPTH, D_FF), 0.01),
        "w_down": nrm(ks[18], (DEPTH, D_FF, D_MODEL), D_FF ** -0.5),
        "g_final": gain(ks[19], (D_MODEL,)),
    }


def reference(x_prompt, x_sample, cache_k_window, cache_v_window, state_conv, state_ffn_conv,
              g_attn_norm, w_in, attn_sinks, conv_w, g_out_attn, g_out_conv, w_out,
              g_ffn_norm, w_gate, w_up, ffn_conv_w, ffn_conv_b, w_down, g_final):
    yp, ys = x_prompt, x_sample
    bp = x_prompt.shape[0]
    kwp, vwp, cvp, ffp = [], [], [], []
    kws, vws, cvs, ffs = [], [], [], []
    for l in range(DEPTH):
        w = (g_attn_norm[l], w_in[l], attn_sinks[l], conv_w[l], g_out_attn[l], g_out_conv[l],
             w_out[l], g_ffn_norm[l], w_gate[l], w_up[l], ffn_conv_w[l], ffn_conv_b[l], w_down[l])
        conv0 = jnp.zeros((bp, CONV_K - 1, CONV_CH), yp.dtype)
        ffn0 = jnp.zeros((bp, CONV_K - 1, D_FF), yp.dtype)
        yp, kp, vp, cp, fp = _layer(yp, _attend_prompt, conv0, ffn0, *w)
        kwp.append(kp[:, -WINDOW:])
        vwp.append(vp[:, -WINDOW:])
        cvp.append(cp)
        ffp.append(fp)
        att_s = functools.partial(_attend_sample, k_buf=cache_k_window[l], v_buf=cache_v_window[l])
        ys, k_s, v_s, c_s, f_s = _layer(ys, att_s, state_conv[l], state_ffn_conv[l], *w)
        kws.append(jnp.concatenate([cache_k_window[l].astype(k_s.dtype), k_s], axis=1)[:, -WINDOW:])
        vws.append(jnp.concatenate([cache_v_window[l].astype(v_s.dtype), v_s], axis=1)[:, -WINDOW:])
        cvs.append(c_s)
        ffs.append(f_s)
    y_prompt = _rmsnorm(yp, g_final)
    y_sample = _rmsnorm(ys, g_final)
    return (y_prompt, y_sample,
            jnp.stack(kwp), jnp.stack(vwp), jnp.stack(cvp), jnp.stack(ffp),
            jnp.stack(kws), jnp.stack(vws), jnp.stack(cvs), jnp.stack(ffs))
```

```python
import numpy as np
from contextlib import ExitStack
import concourse.bass as bass
import concourse.mybir as mybir
from concourse.bass_utils import run_bass_kernel_spmd

F32 = mybir.dt.float32
BF16 = mybir.dt.bfloat16
ALU = mybir.AluOpType
AF = mybir.ActivationFunctionType
AX = mybir.AxisListType
_ESZ = {F32: 4, BF16: 2}

D = 2048
DFF = 5632
NFF = 44
NT = 1282
EPS = 1e-6
NEG = -1e30

PERM_HEADS = [4 * (2 * (qc // 4) + half) + (qc % 4) for qc in range(8) for half in range(2)]


class _Op:
    __slots__ = ("eng", "fn", "deps", "dmadeps", "sig", "semval", "dma", "dsem", "dval", "prev_same_sem")


class Tracker:
    CE = ("pe", "act", "dve", "pool")

    def __init__(self, nc, es, same_engine_sync=True, nq=12):
        self.nc = nc
        self.same = same_engine_sync
        self.eobj = {"pe": nc.tensor, "act": nc.scalar, "dve": nc.vector, "pool": nc.gpsimd, "sp": nc.sync}
        self.ops = []
        self.recs = {}
        self.dram = set()
        self.csem = {e: es.enter_context(nc.semaphore("s_" + e)) for e in self.CE}
        self.nq = nq
        self.qsem = {q: [es.enter_context(nc.semaphore(f"q_{q}{i}")) for i in range(nq)] for q in ("sp", "pool")}
        self.qcnt = {q: 0 for q in self.qsem}
        self.qlast = {q: [None] * nq for q in self.qsem}

    def region(self, ap):
        name = ap.tensor.name
        if name in self.dram:
            return None
        if name.startswith("psb"):
            return (name, 0, 128, 0, 2048)
        esz = _ESZ[ap.dtype]
        pat = ap.ap
        off = int(ap.offset)
        ps, pn = pat[0]
        if ps == 0:
            ps = 1 << 40
        p0 = off // ps
        f0 = off % ps
        ext = 0
        for st, cnt in pat[1:]:
            ext += (cnt - 1) * abs(st)
        return (name, p0, p0 + pn, f0 * esz, (f0 + ext + 1) * esz)

    def add(self, eng, fn, reads, writes, dma=None):
        op = _Op()
        op.eng = eng
        op.fn = fn
        op.dma = dma
        op.sig = False
        op.semval = None
        idx = len(self.ops)
        deps = {}
        dmadeps = set()
        ops = self.ops

        def dep_on(j):
            p = ops[j]
            if p.dma is not None:
                dmadeps.add(j)
            elif deps.get(p.eng, -1) < j:
                deps[p.eng] = j

        rregs = [r for r in (self.region(a) for a in reads if a is not None) if r is not None]
        wregs = [r for r in (self.region(a) for a in writes if a is not None) if r is not None]
        for (name, p0, p1, b0, b1) in rregs:
            psum = name.startswith("psb")
            for rec in self.recs.get(name, ()):
                if (rec[5] or (psum and rec[6] != eng)) and rec[0] < p1 and p0 < rec[1] and rec[2] < b1 and b0 < rec[3]:
                    dep_on(rec[4])
        for (name, p0, p1, b0, b1) in wregs:
            lst = self.recs.get(name, [])
            keep = []
            for rec in lst:
                if rec[0] < p1 and p0 < rec[1] and rec[2] < b1 and b0 < rec[3]:
                    dep_on(rec[4])
                    if rec[0] >= p0 and rec[1] <= p1 and rec[2] >= b0 and rec[3] <= b1:
                        continue
                keep.append(rec)
            keep.append([p0, p1, b0, b1, idx, True, eng])
            self.recs[name] = keep
        for (name, p0, p1, b0, b1) in rregs:
            lst = self.recs.setdefault(name, [])
            found = False
            if dma is None:
                for rec in lst:
                    if (not rec[5]) and rec[6] == eng and rec[0] == p0 and rec[1] == p1 and rec[2] == b0 and rec[3] == b1:
                        rec[4] = idx
                        found = True
                        break
            if not found:
                lst.append([p0, p1, b0, b1, idx, False, eng if dma is None else "dma"])
        fdeps = {}
        for e, j in deps.items():
            if e == eng and dma is None and (e == "pe" or not self.same):
                continue
            fdeps[e] = j
            ops[j].sig = True
        op.deps = fdeps
        op.dmadeps = dmadeps
        if dma is not None:
            q = dma
            c = self.qcnt[q]
            self.qcnt[q] = c + 1
            slot = c % self.nq
            op.dsem = self.qsem[q][slot]
            op.dval = 16 * (c // self.nq + 1)
            op.prev_same_sem = self.qlast[q][slot]
            self.qlast[q][slot] = idx
        ops.append(op)
        return idx

    def emit(self):
        cnt = {e: 0 for e in self.CE}
        for op in self.ops:
            if op.dma is None and op.sig:
                cnt[op.eng] += 1
                op.semval = cnt[op.eng]
        water = {}

        def wait(eng, sem, val):
            key = (eng, id(sem))
            if water.get(key, 0) >= val:
                return
            water[key] = val
            self.eobj[eng].wait_ge(sem, val)

        for op in self.ops:
            eng = op.eng
            for e, j in op.deps.items():
                wait(eng, self.csem[e], self.ops[j].semval)
            for j in op.dmadeps:
                p = self.ops[j]
                wait(eng, p.dsem, p.dval)
            if op.dma is not None and op.prev_same_sem is not None:
                p = self.ops[op.prev_same_sem]
                wait(eng, p.dsem, p.dval)
            inst = op.fn()
            if op.dma is not None:
                inst.then_inc(op.dsem, 16)
            elif op.sig:
                inst.then_inc(self.csem[eng], 1)
        for q in self.qsem:
            for slot in range(self.nq):
                j = self.qlast[q][slot]
                if j is not None:
                    p = self.ops[j]
                    wait("sp", p.dsem, p.dval)
        return cnt

    def dma(self, q, out, in_, after=()):
        eo = self.eobj[q]
        return self.add(q, lambda: eo.dma_start(out=out, in_=in_), [in_, *after], [out], dma=q)

    def mm(self, out, lhsT, rhs, start=True, stop=True):
        nc = self.nc
        return self.add("pe", lambda: nc.tensor.matmul(out, lhsT, rhs, start=start, stop=stop), [lhsT, rhs], [out])

    def tr(self, out, in_, ident):
        nc = self.nc
        return self.add("pe", lambda: nc.tensor.transpose(out, in_, ident), [in_, ident], [out])

    def act(self, out, in_, func, bias=None, scale=1.0, accum_out=None):
        nc = self.nc
        kw = {}
        rd = [in_]
        if bias is not None:
            kw["bias"] = bias
            if not isinstance(bias, (int, float)):
                rd.append(bias)
        if not isinstance(scale, (int, float)):
            rd.append(scale)
        kw["scale"] = scale
        wr = [out]
        if accum_out is not None:
            kw["accum_out"] = accum_out
            wr.append(accum_out)
        return self.add("act", lambda: nc.scalar.activation(out=out, in_=in_, func=func, **kw), rd, wr)

    def tt(self, eng, out, in0, in1, op):
        eo = self.eobj[eng]
        return self.add(eng, lambda: eo.tensor_tensor(out=out, in0=in0, in1=in1, op=op), [in0, in1], [out])

    def ts(self, eng, out, in0, s1, s2, op0, op1=None):
        eo = self.eobj[eng]
        rd = [in0]
        if not isinstance(s1, (int, float)):
            rd.append(s1)
        if s2 is not None and not isinstance(s2, (int, float)):
            rd.append(s2)
        kw = {}
        if op1 is not None:
            kw["op1"] = op1
        return self.add(eng, lambda: eo.tensor_scalar(out=out, in0=in0, scalar1=s1, scalar2=s2, op0=op0, **kw), rd, [out])

    def stt(self, eng, out, in0, scalar, in1, op0, op1):
        eo = self.eobj[eng]
        rd = [in0, in1]
        if not isinstance(scalar, (int, float)):
            rd.append(scalar)
        return self.add(eng, lambda: eo.scalar_tensor_tensor(out=out, in0=in0, scalar=scalar, in1=in1, op0=op0, op1=op1), rd, [out])

    def copy(self, eng, out, in_):
        if eng == "act":
            nc = self.nc
            return self.add("act", lambda: nc.scalar.copy(out=out, in_=in_), [in_], [out])
        eo = self.eobj[eng]
        return self.add(eng, lambda: eo.tensor_copy(out=out, in_=in_), [in_], [out])

    def reduce(self, eng, out, in_, op):
        eo = self.eobj[eng]
        return self.add(eng, lambda: eo.tensor_reduce(out=out, in_=in_, axis=AX.X, op=op), [in_], [out])

    def recip(self, out, in_):
        nc = self.nc
        return self.add("dve", lambda: nc.vector.reciprocal(out=out, in_=in_), [in_], [out])

    def memset(self, eng, ap, val):
        eo = self.eobj[eng]
        return self.add(eng, lambda: eo.memset(ap, val), [], [ap])


V_GA, V_GF, V_GOA, V_GOC, V_CW, V_FW, V_FB, V_N = 0, 16, 32, 40, 48, 72, 204, 248

ARENA_BYTES = 212000
O_CONST = 0
O_JUNK = 4096
O_WR = 8192
O_A = 57344
O_B = 131072
O_C = 175008
assert O_C + 36992 <= ARENA_BYTES


def build_nc(same_engine_sync=True):
    nc = bass.Bass("TRN2", target_bir_lowering=False)
    with ExitStack() as es:
        T = Tracker(nc, es, same_engine_sync=same_engine_sync)

        def DI(name, shape):
            t = nc.dram_tensor(name, list(shape), F32, kind="ExternalInput")
            T.dram.add(t.name)
            return t.ap()

        def DO(name, shape):
            t = nc.dram_tensor(name, list(shape), F32, kind="ExternalOutput")
            T.dram.add(t.name)
            return t.ap()

        xin = DI("xin", (NT, D))
        ck_d = DI("ck", (16, 128, 256))
        cv_d = DI("cv", (16, 128, 256))
        sc_d = DI("sc", (32, 1024))
        sf_d = DI("sf", (32, DFF))
        win_d = DI("win", (D, 4608))
        wout_d = DI("wout", (D, D))
        wg_d = DI("wg", (D, DFF))
        wu_d = DI("wu", (D, DFF))
        wd_d = DI("wd", (DFF, D))
        vecs_d = DI("vecs", (128, V_N))
        sinkbc_d = DI("sinkbc", (128, 16))
        sinkrows_d = DI("sinkrows", (128, 1))
        btab_d = DI("btab", (128, 16 * 257))
        btabs_d = DI("btabs", (128, 137))
        bmini_d = DI("bmini", (2, 16 * 131))
        hmask_d = DI("hmask", (128, 130))
        idn_d = DI("idn", (128, 128))
        gfin_d = DI("gfin", (D,))

        yp_d = DO("yp", (1024, D))
        ys_d = DO("ys", (128, D))
        kwp_d = DO("kwp", (128, 256))
        vwp_d = DO("vwp", (128, 256))
        cvp_d = DO("cvp", (2, 1024))
        ffp_d = DO("ffp", (2, DFF))
        kws_d = DO("kws", (16, 128, 256))
        vws_d = DO("vws", (16, 128, 256))
        cvs_d = DO("cvs", (32, 1024))
        ffs_d = DO("ffs", (32, DFF))

        arena = es.enter_context(nc.sbuf_tensor("arena", [128, ARENA_BYTES // 4], F32))
        psb = [es.enter_context(nc.psum_tensor(f"psb{i}", [128, 512], F32)) for i in range(8)]

        def V(off, shape, dt=F32):
            n = 1
            for s in shape:
                n *= s
            nb = n * _ESZ[dt]
            assert off % 4 == 0 and nb % 4 == 0 and off + nb <= ARENA_BYTES, (off, shape)
            a = arena[:, off // 4:(off + nb) // 4]
            if dt != F32:
                a = a.bitcast(dt)
            if len(shape) == 2:
                a = a.rearrange("p (a b) -> p a b", a=shape[0])
            elif len(shape) == 3:
                a = a.rearrange("p (a b c) -> p a b c", a=shape[0], b=shape[1])
            return a

        def PSB(i):
            return psb[i][:, :].bitcast(BF16)

        o = O_CONST
        ident = V(o, [128]); o += 512
        identb = V(o, [128], BF16); o += 256
        onesf = V(o, [128]); o += 512
        vecs = V(o, [V_N]); o += V_N * 4
        sinkbc = V(o, [16]); o += 64
        sinkrows = V(o, [1]); o += 4
        epsc = V(o, [1]); o += 4
        hmask = V(o, [130]); o += 520
        xTm = V(o, [32]); o += 128
        rinv = V(o, [16]); o += 64
        NSC = 192
        stats = V(o, [NSC]); o += NSC * 4
        ssq2 = V(o, [36]); o += 144
        assert o <= O_JUNK, o
        junk = V(O_JUNK, [2048], BF16)
        scnt = [0]

        def scol():
            i = scnt[0] % NSC
            scnt[0] += 1
            return stats[:, i:i + 1]

        gA = vecs[:, V_GA:V_GA + 16]
        gF = vecs[:, V_GF:V_GF + 16]
        gOA = vecs[:, V_GOA:V_GOA + 8]
        gOC = vecs[:, V_GOC:V_GOC + 8]

        def cw(j, c):
            return vecs[:, V_CW + j * 8 + c:V_CW + j * 8 + c + 1]

        def fw(j, f):
            return vecs[:, V_FW + j * NFF + f:V_FW + j * NFF + f + 1]

        def fb(f):
            return vecs[:, V_FB + f:V_FB + f + 1]

        T.dma("sp", ident, idn_d)
        T.dma("pool", identb, idn_d)
        T.dma("sp", vecs, vecs_d)
        T.dma("sp", sinkbc, sinkbc_d)
        T.dma("sp", sinkrows, sinkrows_d)
        T.dma("sp", hmask, hmask_d)
        T.memset("dve", onesf, 1.0)
        T.memset("dve", epsc, EPS)

        wspecs = []

        def wv16(i, cols):
            return V(O_WR + (i % 3) * 16384, [16, cols], BF16)

        def wv8(i, shape):
            return V(O_WR + (i % 6) * 8192, shape, BF16)

        win_v = win_d.rearrange("(k p) c -> p k c", p=128)
        wout_v = wout_d.rearrange("(k p) c -> p k c", p=128)
        wg_v = wg_d.rearrange("(k p) c -> p k c", p=128)
        wu_v = wu_d.rearrange("(k p) c -> p k c", p=128)
        wd_v = wd_d.rearrange("(c p) n -> p c n", p=128)
        li = 0
        for c in range(8):
            wspecs.append((win_v[:, :, c * 384:(c + 1) * 384], wv16(li, 384))); li += 1
        wspecs.append((win_v[:, :, 3072:3584], wv16(li, 512))); li += 1
        for h in range(2):
            wspecs.append((win_v[:, :, 3584 + h * 512:3584 + (h + 1) * 512], wv16(li, 512))); li += 1
        for nb in range(4):
            wspecs.append((wout_v[:, :, nb * 512:(nb + 1) * 512], wv16(li, 512))); li += 1
        N16 = li
        fi_ = 0
        def _dspecs(grp):
            nonlocal fi_
            for hg in range(2):
                f0 = grp * 4 + hg * 2
                wspecs.append((wd_v[:, f0:f0 + 2, :], wv8(fi_, [2, 2048]))); fi_ += 1

        for grp in range(11):
            for hg in range(2):
                c0 = (grp * 4 + hg * 2) * 128
                wspecs.append((wg_v[:, :, c0:c0 + 256], wv8(fi_, [16, 256]))); fi_ += 1
                wspecs.append((wu_v[:, :, c0:c0 + 256], wv8(fi_, [16, 256]))); fi_ += 1
            if grp >= 1:
                _dspecs(grp - 1)
        _dspecs(10)
        wstate = {"issued": 0, "cons": 0}

        def wget(second=False):
            i = wstate["cons"]
            wstate["cons"] += 1
            floor = i - 1 if second else i
            def sz(n):
                return 16384 if n < N16 else 8192
            while wstate["issued"] < len(wspecs):
                n = wstate["issued"]
                if n > i and sum(sz(m) for m in range(floor, n + 1)) > 49152:
                    break
                src, dst = wspecs[n]
                T.dma("pool", dst, src, after=([xstage[1][:, :]] if n in (1, 2) else ()))
                wstate["issued"] += 1
            return wspecs[i][1]

        hT = V(O_B, [16, NT], BF16)
        xstage = [V(O_C + i * 8192, [2048]) for i in range(4)]
        mixedT = V(O_C, [16, 1156], BF16)
        sconv = V(O_A, [8, 1156])
        rstd_bc = V(O_A + 36992, [1156])
        sqb = [V(O_A + 41616 + i * 2048, [512]) for i in range(2)]
        cu_keep = V(O_A + 45712, [8, 34])
        convout = V(O_A + 46800, [1024])
        eA = O_A + 54648
        cu = V(eA, [NT])
        tmpU = [V(eA + 5128 + i * 2048, [512]) for i in range(2)]
        tmpA = [V(eA + 9224 + i * 2048, [512]) for i in range(2)]
        scT = V(eA + 13320, [8, 32])
        scstage = V(eA + 14344, [1024])
        cus_ext = V(eA + 18440, [16, 10])
        assert eA + 18440 + 640 <= O_A + 73728
        Vc = V(O_A, [16, 256], BF16)
        KTs = V(O_A + 8192, [2, 16, 136], BF16)
        Vnew = V(O_A + 16896, [16, 256], BF16)
        QT = V(O_A + 25088, [8, NT], BF16)
        QSpad = V(O_A + 45600, [2, 16, 128], BF16)
        assert O_A + 45600 + 8192 <= eA
        KT = V(eA, [2, NT], BF16)
        vtm = V(eA + 5128, [11, 256], BF16)
        kc_tm = V(eA + 10760, [16, 256], BF16)
        assert eA + 10760 + 8192 <= O_A + 73728
        kvstage = [V(O_JUNK + i * 2048, [512]) for i in range(2)]
        btab = V(O_B, [16, 257])
        btabs = V(O_B + 16448, [137])
        bmini = V(O_B + 16448 + 548, [16, 131])
        ob = O_B + 16448 + 548 + 8384
        S_sb = [V(ob + i * 1032, [258]) for i in range(2)]; ob += 2064
        P_sb = [V(ob + i * 520, [260], BF16) for i in range(2)]; ob += 1040
        Pf_sb = [V(ob + i * 552, [138]) for i in range(2)]; ob += 1104
        PT_sb = [V(ob + i * 512, [256], BF16) for i in range(2)]; ob += 1024
        o_attn_sb = ob
        attn_sb = V(ob, [1024]); ob += 4096
        attn_sc = V(ob, [1024]); ob += 4096
        rb_s = V(ob, [128]); ob += 512
        rs_all = [V(ob + i * 64, [16]) for i in range(2)]; ob += 128
        ng_all = [V(ob + i * 64, [16]) for i in range(2)]; ob += 128
        assert ob <= O_B + 41024, ob
        attnT_s = V(eA + 10760, [8, 128])
        sq_s = V(eA + 10760 + 4096, [1024])
        x2acc = V(O_A, [9, 2048])
        h2T = V(O_B, [16, 1154], BF16)
        xres = [V(O_JUNK + i * 2048, [512]) for i in range(2)]
        xs2b = [V(O_C + i * 8192, [2048]) for i in range(2)]
        x2Tm = V(O_B + 36928, [32])
        sqm = V(O_B + 36928 + 128, [32])
        rm = V(O_B + 36928 + 256, [2])
        actb = [V(O_C + i * 9216, [4, 1152], BF16) for i in range(2)]
        g_sb = V(O_C + 18432, [NT])
        tAb = [V(O_C + 23560 + i * 2048, [512]) for i in range(2)]
        tSb = V(O_C + 27656, [512])
        sfT = V(O_C + 29704, [NFF, 32])
        gs_ext = V(O_C + 35336, [16, 10])
        assert O_C + 35336 + 640 <= ARENA_BYTES
        ffn_keep = V(O_B + 36928 + 512, [NFF, 34])
        assert O_B + 36928 + 512 + NFF * 34 * 4 <= O_C
        sfstage = V(O_C, [1408])
        ffstage = V(O_B + 16384, [2048])

        def bc_mid(ap, n):
            return ap.unsqueeze(2).to_broadcast([ap.shape[0], ap.shape[1], n])

        def rstd_of(ssq_ap, rows, scale):
            a = scol()
            b = scol()
            T.act(a[0:rows], ssq_ap, AF.Sqrt, bias=epsc[0:rows], scale=scale)
            T.recip(b[0:rows], a[0:rows])
            return b

        tilesA = [(0, 2)] + [(2 + 128 * i, 130 + 128 * i) for i in range(9)] + [(1154, 1282)]
        T.dma("sp", xstage[3][0:2, :], xin[128:130, :])
        for k in range(16):
            T.tr(psb[7][:, 2 * k:2 * k + 2], xstage[3][0:2, k * 128:(k + 1) * 128], ident[0:2, 0:2])
        T.copy("dve", xTm, psb[7][:, 0:32])
        def norm1_tile(i):
            c0, c1 = tilesA[i]
            rows = c1 - c0
            xs_ = xstage[i % 4]
            T.dma("sp", xs_[0:rows, :], xin[c0:c1, :])
            ssq = scol()
            T.act(junk[0:rows, :], xs_[0:rows, :], AF.Square, accum_out=ssq[0:rows])
            rs = rstd_of(ssq[0:rows], rows, 1.0 / D)
            T.act(xs_[0:rows, :], xs_[0:rows, :], AF.Copy, scale=rs[0:rows])
            for kb in range(4):
                bank = (i % 2) * 4 + kb
                for j in range(4):
                    k = kb * 4 + j
                    T.tr(psb[bank][:, j * 128:j * 128 + rows], xs_[0:rows, k * 128:(k + 1) * 128], ident[0:rows, 0:rows])
                T.tt("dve", hT[:, kb * 4:kb * 4 + 4, c0:c1],
                     psb[bank][:, :].rearrange("p (a b) -> p a b", a=4)[:, :, 0:rows],
                     bc_mid(gA[:, kb * 4:kb * 4 + 4], rows), ALU.mult)

        for i in range(6):
            norm1_tile(i)
        T.dma("sp", kws_d[:, 0:120, :], ck_d[:, 8:128, :])
        T.dma("sp", vws_d[:, 0:120, :], cv_d[:, 8:128, :])
        T.dma("sp", scstage[0:32, :], sc_d)
        for c in range(8):
            T.tr(psb[6][:, c * 32:(c + 1) * 32], scstage[0:32, c * 128:(c + 1) * 128], ident[0:32, 0:32])
        T.copy("dve", scT.rearrange("p a b -> p (a b)"), psb[6][:, 0:256])
        NTQ = [(126, 638), (638, 1150), (1150, 1282)]
        it = 0
        for c in range(8):
            slot = wget()
            for ti, (n0, n1) in enumerate(NTQ):
                if c == 0 and ti == 1:
                    for i in range(6, 10):
                        norm1_tile(i)
                if c == 0 and ti == 2:
                    norm1_tile(10)
                W = n1 - n0
                pB, pC, pU = psb[(3 * it) % 8], psb[(3 * it + 1) % 8], psb[(3 * it + 2) % 8]
                for wi, pX in enumerate((pB, pC, pU)):
                    for k in range(16):
                        T.mm(pX[:, 0:W], slot[:, k, wi * 128:(wi + 1) * 128], hT[:, k, n0:n1], start=(k == 0), stop=(k == 15))
                tU = tmpU[it % 2][:, 0:W]
                T.copy("act", tU, pU[:, 0:W])
                T.tt("dve", cu[:, n0:n1], pC[:, 0:W], tU, ALU.mult)
                lo, hi = max(n0, 128), min(n1, 1154)
                if hi > lo:
                    L = hi - lo
                    tA = tmpA[it % 2][:, 0:L]
                    T.act(tA, cu[:, lo - 2:hi - 2], AF.Copy, scale=cw(0, c))
                    T.stt("dve", tA, cu[:, lo - 1:hi - 1], cw(1, c), tA, ALU.mult, ALU.add)
                    T.stt("dve", tA, cu[:, lo:hi], cw(2, c), tA, ALU.mult, ALU.add)
                    T.tt("dve", sconv[:, c, lo - 126:hi - 126], pB[:, lo - n0:hi - n0], tA, ALU.mult)
                if ti == 2:
                    T.copy("dve", cus_ext[:, :, 2:10], cu[:, 1154:1282].rearrange("p (s t) -> p s t", s=16))
                    T.copy("act", cus_ext[:, :, 0:2], scT[:, c, :].rearrange("p (s r) -> p s r", s=16))
                    tA3 = tmpA[(it + 1) % 2][:, 0:128].rearrange("p (s t) -> p s t", s=16)
                    T.act(tA3, cus_ext[:, :, 0:8], AF.Copy, scale=cw(0, c))
                    T.stt("dve", tA3, cus_ext[:, :, 1:9], cw(1, c), tA3, ALU.mult, ALU.add)
                    T.stt("dve", tA3, cus_ext[:, :, 2:10], cw(2, c), tA3, ALU.mult, ALU.add)
                    T.tt("dve", sconv[:, c, 1028:1156].rearrange("p (s t) -> p s t", s=16),
                         pB[:, 4:132].rearrange("p (s t) -> p s t", s=16), tA3, ALU.mult)
                    T.copy("act", cu_keep[:, c, 0:2], cu[:, 1152:1154])
                    T.copy("act", cu_keep[:, c, 2:34].rearrange("p (s r) -> p s r", s=16), cus_ext[:, :, 8:10])
                it += 1
        MT = [(2, 514), (514, 1026), (1026, 1156)]
        it = 0
        for c in range(8):
            for ti, (m0, m1) in enumerate(MT):
                sq = sqb[it % 2][:, 0:m1 - m0]
                T.act(sq, sconv[:, c, m0:m1], AF.Square)
                T.mm(psb[ti][:, 0:m1 - m0], onesf, sq, start=(c == 0), stop=(c == 7))
                it += 1
        for ti, (m0, m1) in enumerate(MT):
            T.act(rstd_bc[:, m0:m1], psb[ti][:, 0:m1 - m0], AF.Sqrt, bias=epsc, scale=1.0 / 1024)
            T.recip(rstd_bc[:, m0:m1], rstd_bc[:, m0:m1])
        for c in range(8):
            T.stt("dve", mixedT[:, 8 + c, 2:1156], sconv[:, c, 2:1156], gOC[:, c:c + 1],
                  rstd_bc[:, 2:1156], ALU.mult, ALU.mult)
        for c in range(8):
            T.tr(psb[6 + c // 4][0:34, (c % 4) * 128:(c % 4 + 1) * 128], cu_keep[:, c, :], ident)
        T.copy("act", convout[0:34, 0:512], psb[6][0:34, :])
        T.copy("act", convout[0:34, 512:1024], psb[7][0:34, :])
        T.dma("sp", cvp_d, convout[0:2, :])
        T.dma("sp", cvs_d, convout[2:34, :])

        slot = wget()
        T.dma("pool", kc_tm, ck_d.rearrange("s k d -> k s d"))
        NTK = [(0, 512), (512, 1024), (1024, 1282)]
        it = 0
        for pc in range(2):
            for ti, (n0, n1) in enumerate(NTK):
                W = n1 - n0
                ps = psb[it % 4]
                for k in range(16):
                    T.mm(ps[:, 0:W], slot[:, k, pc * 128:(pc + 1) * 128], hT[:, k, n0:n1], start=(k == 0), stop=(k == 15))
                T.copy("act" if it % 2 == 0 else "dve", KT[:, pc, n0:n1], ps[:, 0:W])
                it += 1
        for i, (c0, c1) in enumerate(tilesA):
            rows = c1 - c0
            full = i in (9, 10)
            ps = psb[4 + i % 2]
            r0 = 0 if full else 256
            N = 512 - r0
            for k in range(16):
                T.mm(ps[0:rows, 0:N], hT[:, k, c0:c1], slot[:, k, r0:512], start=(k == 0), stop=(k == 15))
            if full:
                stg = kvstage[i % 2]
                T.copy("dve", stg, ps[:, 0:512])
                T.copy("act", vtm[:, i, :], ps[:, 256:512])
                if i == 9:
                    T.dma("sp", kwp_d, stg[:, 0:256])
                    T.dma("sp", vwp_d, stg[:, 256:512])
                else:
                    for s in range(16):
                        T.dma("sp", kws_d[s, 120:128, :], stg[s * 8:(s + 1) * 8, 0:256])
                        T.dma("sp", vws_d[s, 120:128, :], stg[s * 8:(s + 1) * 8, 256:512])
            else:
                T.copy("act" if i % 2 == 0 else "dve", vtm[0:rows, i, :], ps[0:rows, 0:256])
        for sg in range(4):
            pv = PSB(6 + sg % 2)
            for si in range(4):
                s = sg * 4 + si
                for pr in range(2):
                    T.tr(pv[:, (si * 2 + pr) * 128:(si * 2 + pr + 1) * 128], kc_tm[:, s, pr * 128:(pr + 1) * 128], identb)
            T.copy("dve" if sg % 2 == 0 else "act",
                   KTs[:, :, sg * 4:(sg + 1) * 4, 0:128].rearrange("p pr s k -> p s pr k"),
                   pv.rearrange("p (s pr k) -> p s pr k", s=4, pr=2))
        for pr in range(2):
            T.copy("dve", KTs[:, pr, :, 128:136], KT[:, pr, 1154:1282].rearrange("p (s t) -> p s t", s=16))
        for s in range(16):
            T.dma("sp", Vnew[0:8, s, :], vtm[s * 8:(s + 1) * 8, 10, :])

        T.memset("pool", QSpad.rearrange("p a b c -> p (a b c)"), 0.0)
        it = 0
        for hl in range(2):
            slot = wget()
            for jj in range(4):
                qc = hl * 4 + jj
                for ti, (n0, n1) in enumerate(NTQ):
                    W = n1 - n0
                    ps = psb[it % 4]
                    for k in range(16):
                        T.mm(ps[:, 0:W], slot[:, k, jj * 128:(jj + 1) * 128], hT[:, k, n0:n1], start=(k == 0), stop=(k == 15))
                    T.copy("act" if it % 2 == 0 else "dve", QT[:, qc, n0:n1], ps[:, 0:W])
                    if ti == 2:
                        for half in range(2):
                            j = 2 * (qc // 4) + half
                            g = qc % 4
                            T.copy("dve", QSpad[half * 64:(half + 1) * 64, qc // 4, :, 32 * j + 8 * g:32 * j + 8 * g + 8],
                                   ps[half * 64:(half + 1) * 64, 4:132].rearrange("p (s t) -> p s t", s=16))
                    it += 1
        T.dma("pool", Vc, cv_d.rearrange("s k d -> k s d"))
        T.dma("sp", bmini[0:2].rearrange("p a b -> p (a b)"), bmini_d)
        btab_dv = btab_d.rearrange("p (a b) -> p a b", a=16)
        for hg in range(4):
            T.dma("sp", btab[:, hg * 4:(hg + 1) * 4, :], btab_dv[:, hg * 4:(hg + 1) * 4, :])
        T.dma("sp", btabs, btabs_d)

        T.memset("dve", psb[0][:, :], 0.0)
        T.memset("dve", psb[1][:, :], 0.0)
        blocks = []
        blocks.append(dict(rows=2, q0=128, k0=0, nk=130, segs=[(0, 2, vtm[:, 0, :]), (2, 130, vtm[:, 1, :])],
                           bias=lambda e: bmini[0:2, e, :], mask=hmask[0:2, 0:130], mcol=2))
        for n in range(1, 9):
            q0 = 130 + 128 * (n - 1)
            blocks.append(dict(rows=128, q0=q0, k0=q0 - 128, nk=256,
                               segs=[(0, 128, vtm[:, n, :]), (128, 256, vtm[:, n + 1, :])],
                               bias=lambda e: btab[:, e, :], mask=(hmask[:, 0:128] if n == 1 else None), mcol=q0 - 126))
        units = [(bi, e) for bi in range(len(blocks)) for e in range(16)]
        NU = len(units)

        def st_A(u):
            bi, e = units[u]
            b_ = blocks[bi]
            rows, nk = b_["rows"], b_["nk"]
            qc, half = e // 2, e % 2
            pb = half * 64
            T.mm(psb[u % 2][0:rows, 0:nk], QT[pb:pb + 64, qc, b_["q0"]:b_["q0"] + rows],
                 KT[pb:pb + 64, qc // 4, b_["k0"]:b_["k0"] + nk])

        def st_B(u):
            bi, e = units[u]
            b_ = blocks[bi]
            rows, nk = b_["rows"], b_["nk"]
            Ssb = S_sb[u % 2][0:rows, 0:nk + 1]
            T.stt("dve", Ssb, psb[u % 2][0:rows, 0:nk + 1], 0.125, b_["bias"](e), ALU.mult, ALU.add)
            if b_["mask"] is not None:
                mk = b_["mask"].shape[1]
                T.tt("dve", Ssb[:, 0:mk], Ssb[:, 0:mk], b_["mask"], ALU.add)
            T.add("dve", lambda o=ng_all[bi % 2][0:rows, e:e + 1], i=Ssb: nc.vector.tensor_reduce(
                out=o, in_=i, axis=AX.X, op=ALU.max, negate=True), [Ssb], [ng_all[bi % 2][0:rows, e:e + 1]])

        def st_C(u):
            bi, e = units[u]
            b_ = blocks[bi]
            rows, nk = b_["rows"], b_["nk"]
            T.act(P_sb[u % 2][0:rows, 0:nk + 1], S_sb[u % 2][0:rows, 0:nk + 1], AF.Exp, bias=ng_all[bi % 2][0:rows, e:e + 1],
                  accum_out=rs_all[bi % 2][0:rows, e:e + 1])

        def st_E(u):
            bi, e = units[u]
            b_ = blocks[bi]
            rows = b_["rows"]
            PTp = PSB(2 + u % 2)
            for si, (k0, k1, Vap) in enumerate(b_["segs"]):
                T.tr(PTp[0:k1 - k0, si * 128:si * 128 + rows], P_sb[u % 2][0:rows, k0:k1], identb[0:rows, 0:rows])

        def st_F(u):
            bi, e = units[u]
            b_ = blocks[bi]
            rows = b_["rows"]
            PTp = PSB(2 + u % 2)
            if rows == 128 and all(k1 - k0 == 128 for (k0, k1, _) in b_["segs"]):
                ns = len(b_["segs"])
                T.copy("act", PT_sb[u % 2][:, 0:128 * ns], PTp[:, 0:128 * ns])
                return
            for si, (k0, k1, Vap) in enumerate(b_["segs"]):
                T.copy("act", PT_sb[u % 2][0:k1 - k0, si * 128:si * 128 + rows],
                       PTp[0:k1 - k0, si * 128:si * 128 + rows])

        def st_G(u):
            bi, e = units[u]
            b_ = blocks[bi]
            rows = b_["rows"]
            j = 2 * ((e // 2) // 4) + e % 2
            ops_ = psb[4 + 2 * (bi % 2) + e // 8][0:rows, (e % 8) * 64:(e % 8 + 1) * 64]
            ns = len(b_["segs"])
            for si, (k0, k1, Vap) in enumerate(b_["segs"]):
                T.mm(ops_, PT_sb[u % 2][0:k1 - k0, si * 128:si * 128 + rows], Vap[0:k1 - k0, j * 64:(j + 1) * 64],
                     start=(si == 0), stop=(si == ns - 1))

        p1st = {}

        def post1a(bi):
            rows = blocks[bi]["rows"]
            T.recip(rinv[0:rows], rs_all[bi % 2][0:rows])
            A_sb = attn_sb[0:rows]
            for bnk in range(2):
                T.tt("dve", A_sb[:, bnk * 512:(bnk + 1) * 512].rearrange("p (h d) -> p h d", h=8),
                     psb[4 + 2 * (bi % 2) + bnk][0:rows, :].rearrange("p (h d) -> p h d", h=8),
                     bc_mid(rinv[0:rows, bnk * 8:(bnk + 1) * 8], 64), ALU.mult)

        def post1b(bi):
            rows = blocks[bi]["rows"]
            ssq = scol()
            T.act(junk[0:rows, 0:1024], attn_sb[0:rows], AF.Square, accum_out=ssq[0:rows])
            p1st[bi] = ssq

        def post1c(bi):
            rows = blocks[bi]["rows"]
            a = scol()
            T.act(a[0:rows], p1st[bi][0:rows], AF.Sqrt, bias=epsc[0:rows], scale=1.0 / 1024)
            p1st[bi] = a

        def post1d(bi):
            rows = blocks[bi]["rows"]
            b = scol()
            T.recip(b[0:rows], p1st[bi][0:rows])
            T.tt("pool", attn_sc[0:rows], attn_sb[0:rows], b[0:rows, 0:1].to_broadcast([rows, 1024]), ALU.mult)

        def post2(bi):
            b_ = blocks[bi]
            rows, mcol = b_["rows"], b_["mcol"]
            for bnk in range(2):
                pbk = psb[4 + 2 * (bi % 2) + bnk]
                for jj in range(4):
                    c = bnk * 4 + jj
                    T.tr(pbk[:, jj * 128:jj * 128 + rows], attn_sc[0:rows, c * 128:(c + 1) * 128], ident[0:rows, 0:rows])
                T.tt("dve", mixedT[:, bnk * 4:bnk * 4 + 4, mcol:mcol + rows],
                     pbk[:, :].rearrange("p (a b) -> p a b", a=4)[:, :, 0:rows],
                     bc_mid(gOA[:, bnk * 4:bnk * 4 + 4], rows), ALU.mult)

        NSU = 16
        sden = {}

        def ss_A(s):
            for pr in range(2):
                T.mm(psb[s % 2][:, 300:436], QSpad[:, pr, s, :], KTs[:, pr, s, :], start=(pr == 0), stop=(pr == 1))

        def ss_B(s):
            Ssb = S_sb[s % 2][:, 0:137]
            T.stt("dve", Ssb, psb[s % 2][:, 300:437], 0.125, btabs, ALU.mult, ALU.add)
            negm = scol()
            T.add("dve", lambda o=negm, i=Ssb: nc.vector.tensor_reduce(out=o, in_=i, axis=AX.X, op=ALU.max, negate=True),
                  [Ssb], [negm])
            sden[s] = negm

        def ss_C(s):
            negm = sden[s]
            rsum = scol()
            T.act(Pf_sb[s % 2][:, 0:137], S_sb[s % 2][:, 0:137], AF.Exp, bias=negm, accum_out=rsum)
            sden[s] = rsum

        def ss_D(s):
            rv = scol()
            T.recip(rv, sden[s])
            T.ts("dve", P_sb[s % 2][:, 0:136], Pf_sb[s % 2][:, 0:136], rv, None, ALU.mult)

        def ss_E(s):
            PTp = PSB(2 + s % 2)
            Pn = P_sb[s % 2][:, 0:136]
            T.tr(PTp[:, 0:128], Pn[:, 0:128], identb)
            T.tr(PTp[0:8, 128:256], Pn[:, 128:136], identb)

        def ss_F(s):
            PTp = PSB(2 + s % 2)
            T.copy("act", PT_sb[s % 2][:, 0:128], PTp[:, 0:128])
            T.copy("act", PT_sb[s % 2][0:8, 128:256], PTp[0:8, 128:256])

        def ss_G(s):
            Ops = psb[sbank0 + s % 2]
            PTs = PT_sb[s % 2]
            for pr in range(2):
                T.mm(Ops[:, pr * 64:(pr + 1) * 64], Vc[:, s, pr * 128:(pr + 1) * 128], PTs[:, pr * 64:(pr + 1) * 64],
                     start=True, stop=False)
                T.mm(Ops[:, pr * 64:(pr + 1) * 64], Vnew[0:8, s, pr * 128:(pr + 1) * 128],
                     PTs[0:8, 128 + pr * 64:128 + (pr + 1) * 64], start=False, stop=True)

        def ss_H(s):
            Ops = psb[sbank0 + s % 2]
            for half in range(2):
                src = Ops[half * 64:(half + 1) * 64, 0:128].rearrange("p (pr h g t) -> p pr h g t", pr=2, h=2, g=4)[:, :, half, :, :]
                dst = attnT_s[half * 64:(half + 1) * 64, :, 8 * s:8 * s + 8].rearrange("p (pr g) t -> p pr g t", pr=2)
                T.copy("dve", dst, src)

        sbank0 = 4 + 2 * (1 - (len(blocks) - 1) % 2) if blocks else 4

        def sample_iter(t):
            if t == 0 and NSU:
                ss_A(0)
            if t + 1 < NSU:
                ss_A(t + 1)
            if t < NSU:
                ss_B(t)
                ss_C(t)
            if 0 <= t - 1 < NSU:
                ss_D(t - 1)
                ss_E(t - 1)
                ss_F(t - 1)
            if 0 <= t - 2 < NSU:
                ss_G(t - 2)
            if 0 <= t - 3 < NSU:
                ss_H(t - 3)

        pend = {}
        if NU:
            st_A(0)
        for t in range(NU + 10):
            if t + 1 < NU:
                st_A(t + 1)
            if t < NU:
                st_B(t)
                st_C(t)
            if 0 <= t - 1 < NU:
                st_E(t - 1)
                st_F(t - 1)
            if 0 <= t - 2 < NU:
                st_G(t - 2)
                bi, e = units[t - 2]
                if e == 15:
                    pend.setdefault(t, []).append((post1a, bi))
                    pend.setdefault(t + 1, []).append((post1b, bi))
                    pend.setdefault(t + 2, []).append((post1c, bi))
                    pend.setdefault(t + 5, []).append((post1d, bi))
                    pend.setdefault(t + 7, []).append((post2, bi))
            for fn_, bi_ in pend.pop(t, []):
                fn_(bi_)
            if t >= NU:
                sample_iter(t - NU)
        assert not pend
        for t in range(10, NSU + 4):
            sample_iter(t)
        T.act(sq_s, attnT_s.rearrange("p a b -> p (a b)"), AF.Square)
        for c in range(8):
            T.mm(psb[6][:, 0:128], onesf, sq_s[:, c * 128:(c + 1) * 128], start=(c == 0), stop=(c == 7))
        T.act(rb_s, psb[6][:, 0:128], AF.Sqrt, bias=epsc, scale=1.0 / 1024)
        T.recip(rb_s, rb_s)
        for c in range(8):
            T.stt("dve", mixedT[:, c, 1028:1156], attnT_s[:, c, :], gOA[:, c:c + 1], rb_s, ALU.mult, ALU.mult)

        rt = [(130 + 128 * i, 258 + 128 * i) for i in range(8)] + [(1154, 1282)]
        dgb = [V(O_B + 37440 + i * 512, [128]) for i in range(2)]
        rbcb = [V(O_B + 37440 + 1024 + i * 512, [128]) for i in range(2)]
        dummy6 = V(O_B + 37440 + 2048, [512], BF16)
        n2 = {"g": 0, "e": 0, "slot": 0}
        n2q = []
        evt = [V(O_B + 37440 + 3072 + i * 512, [128]) for i in range(4)]

        n2rs = {}

        def norm2_stats(r):
            ssq = scol()
            T.reduce("dve", ssq, ssq2[:, 4 * r:4 * r + 4], ALU.add)
            rs = rstd_of(ssq, 128, 1.0 / D)
            T.ts("dve", dgb[r % 2], ident, rs, None, ALU.mult)

        def norm2_grp(r, kb):
            c0, c1 = rt[r]
            rb = rbcb[r % 2]
            if kb == 0:
                T.mm(psb[6][:, 0:128], onesf, dgb[r % 2])
                T.copy("act", rb, psb[6][:, 0:128])
            bank = 3 + n2["g"] % 3
            n2["g"] += 1
            for j in range(4):
                k = kb * 4 + j
                T.tr(psb[bank][:, j * 128:(j + 1) * 128], x2acc[:, r, k * 128:(k + 1) * 128], ident)
            for j in range(4):
                k = kb * 4 + j
                if kb % 2 == 0:
                    T.stt("dve", h2T[:, k, c0 - 128:c1 - 128], psb[bank][:, j * 128:(j + 1) * 128], gF[:, k:k + 1], rb,
                          ALU.mult, ALU.mult)
                else:
                    tmp = evt[n2["e"] % 4]
                    n2["e"] += 1
                    T.act(tmp, psb[bank][:, j * 128:(j + 1) * 128], AF.Copy, scale=gF[:, k:k + 1])
                    T.tt("pool", h2T[:, k, c0 - 128:c1 - 128], tmp, rb, ALU.mult)

        it = 0
        for nb in range(4):
            slot = wget()
            for jj in range(4):
                oc = nb * 4 + jj
                for k in range(16):
                    T.mm(psb[7][:, oc * 2:oc * 2 + 2], slot[:, k, jj * 128:(jj + 1) * 128], mixedT[:, k, 2:4],
                         start=(k == 0), stop=(k == 15))
            for r, (c0, c1) in enumerate(rt):
                ps = psb[it % 3]
                xr = xres[it % 2]
                T.dma("sp", xr, xin[c0:c1, nb * 512:(nb + 1) * 512])
                for k in range(16):
                    T.mm(ps[:, :], mixedT[:, k, c0 - 126:c1 - 126], slot[:, k, :], start=(k == 0), stop=(k == 15))
                    if nb == 3 and k % 4 == 3:
                        n2["slot"] += 1
                        if n2q and n2q[0][2] <= n2["slot"]:
                            g_ = n2q.pop(0)
                            norm2_grp(g_[0], g_[1])
                T.tt("dve", x2acc[:, r, nb * 512:(nb + 1) * 512], ps[:, :], xr, ALU.add)
                T.act(dummy6, x2acc[:, r, nb * 512:(nb + 1) * 512], AF.Square, accum_out=ssq2[:, 4 * r + nb:4 * r + nb + 1])
                if nb == 3:
                    norm2_stats(r)
                    for kb in range(4):
                        n2q.append((r, kb, n2["slot"] + 3))
                it += 1
        for g_ in n2q:
            norm2_grp(g_[0], g_[1])
        T.tt("dve", x2Tm, psb[7][:, 0:32], xTm, ALU.add)
        T.act(sqm, x2Tm, AF.Square)
        for k in range(16):
            T.mm(psb[6][:, 0:2], onesf, sqm[:, 2 * k:2 * k + 2], start=(k == 0), stop=(k == 15))
        T.act(rm, psb[6][:, 0:2], AF.Sqrt, bias=epsc, scale=1.0 / D)
        T.recip(rm, rm)
        x2Tm3 = x2Tm.rearrange("p (k t) -> p k t", k=16)
        T.tt("dve", x2Tm3, x2Tm3, rm.unsqueeze(1).to_broadcast([128, 16, 2]), ALU.mult)
        T.tt("dve", h2T[:, :, 0:2], x2Tm3, bc_mid(gF, 2), ALU.mult)

        sfst = [V(O_C + piece * 5632, [1408]) for piece in range(4)]
        for piece in range(4):
            T.dma("sp", sfst[piece][0:32, :], sf_d[:, piece * 1408:(piece + 1) * 1408])
        for piece in range(4):
            sfstage = sfst[piece]
            pbk = psb[4 + piece % 2]
            for cc in range(11):
                T.tr(pbk[:, cc * 32:(cc + 1) * 32], sfstage[0:32, cc * 128:(cc + 1) * 128], ident[0:32, 0:32])
            T.copy("dve", sfT[:, piece * 11:(piece + 1) * 11, :].rearrange("p a b -> p (a b)"), pbk[:, 0:352])
        NTF = [(128, 640), (640, 1152), (1152, 1282)]
        gstate = {"it": 0, "dit": 0}

        def gateup(grp):
            for hg in range(2):
                gslot = wget()
                uslot = wget(second=True)
                for ci in range(2):
                    f = grp * 4 + hg * 2 + ci
                    fi = hg * 2 + ci
                    ab = actb[grp % 2]
                    for ti, (n0, n1) in enumerate(NTF):
                        it = gstate["it"]
                        gstate["it"] += 1
                        W = n1 - n0
                        s0 = (it % 2) * 2
                        pG, pU = psb[s0], psb[s0 + 1]
                        for k in range(16):
                            T.mm(pG[:, 0:W], gslot[:, k, ci * 128:(ci + 1) * 128], h2T[:, k, n0 - 128:n1 - 128],
                                 start=(k == 0), stop=(k == 15))
                        for k in range(16):
                            T.mm(pU[:, 0:W], uslot[:, k, ci * 128:(ci + 1) * 128], h2T[:, k, n0 - 128:n1 - 128],
                                 start=(k == 0), stop=(k == 15))
                        T.copy("act", g_sb[:, n0:n1], pG[:, 0:W])
                        lo, hi = max(n0, 130), min(n1, 1154)
                        if hi > lo:
                            L = hi - lo
                            tA = tAb[it % 2][:, 0:L]
                            T.act(tA, g_sb[:, lo - 2:hi - 2], AF.Copy, scale=fw(0, f))
                            T.stt("dve", tA, g_sb[:, lo - 1:hi - 1], fw(1, f), tA, ALU.mult, ALU.add)
                            T.stt("dve", tA, g_sb[:, lo:hi], fw(2, f), tA, ALU.mult, ALU.add)
                            tS = tSb[:, 0:L]
                            T.act(tS, tA, AF.Silu, bias=fb(f))
                            T.tt("dve", ab[:, fi, lo - 130:hi - 130], pU[:, lo - n0:hi - n0], tS, ALU.mult)
                        if ti == 2:
                            T.copy("dve", gs_ext[:, :, 2:10], g_sb[:, 1154:1282].rearrange("p (s t) -> p s t", s=16))
                            T.copy("act", gs_ext[:, :, 0:2], sfT[:, f, :].rearrange("p (s r) -> p s r", s=16))
                            tA3 = tAb[(it + 1) % 2][:, 0:128].rearrange("p (s t) -> p s t", s=16)
                            T.act(tA3, gs_ext[:, :, 0:8], AF.Copy, scale=fw(0, f))
                            T.stt("dve", tA3, gs_ext[:, :, 1:9], fw(1, f), tA3, ALU.mult, ALU.add)
                            T.stt("dve", tA3, gs_ext[:, :, 2:10], fw(2, f), tA3, ALU.mult, ALU.add)
                            tS3 = tSb[:, 0:128].rearrange("p (s t) -> p s t", s=16)
                            T.act(tS3, tA3, AF.Silu, bias=fb(f))
                            T.tt("dve", ab[:, fi, 1024:1152].rearrange("p (s t) -> p s t", s=16),
                                 pU[:, 2:130].rearrange("p (s t) -> p s t", s=16), tS3, ALU.mult)
                            T.copy("act", ffn_keep[:, f, 0:2], g_sb[:, 1152:1154])
                            T.copy("act", ffn_keep[:, f, 2:34].rearrange("p (s r) -> p s r", s=16), gs_ext[:, :, 8:10])

        gfin_bc = V(O_B, [2048])
        dummy8 = V(O_B + 8192, [512], BF16)

        def final_tile(r):
            x2 = x2acc[:, r, :]
            ssq = scol()
            T.reduce("dve", ssq, ssq2[:, 4 * r:4 * r + 4], ALU.add)
            rs = rstd_of(ssq, 128, 1.0 / D)
            T.stt("dve", x2[:, 0:1024], x2[:, 0:1024], rs, gfin_bc[:, 0:1024], ALU.mult, ALU.mult)
            T.act(x2[:, 1024:2048], x2[:, 1024:2048], AF.Copy, scale=rs)
            T.tt("pool", x2[:, 1024:2048], x2[:, 1024:2048], gfin_bc[:, 1024:2048], ALU.mult)
            if r < 8:
                T.dma("sp", yp_d[r * 128:(r + 1) * 128, :], x2)
            else:
                T.dma("sp", ys_d, x2)

        def down(grp, last=False):
            dsl = [wget(), wget(second=True)]
            ab = actb[grp % 2]
            if last:
                T.dma("sp", gfin_bc, gfin_d.partition_broadcast(128))
            for r in range(9):
                for nb in range(4):
                    ps = psb[4 + gstate["dit"] % 4]
                    gstate["dit"] += 1
                    for fi in range(4):
                        T.mm(ps[:, :], ab[:, fi, r * 128:(r + 1) * 128], dsl[fi // 2][:, fi % 2, nb * 512:(nb + 1) * 512],
                             start=(fi == 0), stop=(fi == 3))
                    T.tt("dve", x2acc[:, r, nb * 512:(nb + 1) * 512], ps[:, :], x2acc[:, r, nb * 512:(nb + 1) * 512], ALU.add)
                    if last:
                        T.act(dummy8, x2acc[:, r, nb * 512:(nb + 1) * 512], AF.Square,
                              accum_out=ssq2[:, 4 * r + nb:4 * r + nb + 1])
                if last and r >= 1:
                    final_tile(r - 1)
            if last:
                final_tile(8)

        for grp in range(11):
            gateup(grp)
            if grp >= 1:
                down(grp - 1)
        for rnd in range(3):
            chunks = list(range(rnd * 16, min(NFF, rnd * 16 + 16)))
            for cc, f in enumerate(chunks):
                T.tr(psb[cc // 4][0:34, (cc % 4) * 128:(cc % 4 + 1) * 128], ffn_keep[:, f, :], ident)
            nbk = (len(chunks) + 3) // 4
            for b in range(nbk):
                T.copy("act" if b % 2 == 0 else "dve", ffstage[0:34, b * 512:(b + 1) * 512], psb[b][0:34, :])
            ncols = len(chunks) * 128
            T.dma("sp", ffp_d[:, rnd * 2048:rnd * 2048 + ncols], ffstage[0:2, 0:ncols])
            T.dma("sp", ffs_d[:, rnd * 2048:rnd * 2048 + ncols], ffstage[2:34, 0:ncols])

        down(10, last=True)
        assert wstate["cons"] == len(wspecs), (wstate, len(wspecs))
        T.emit()
    return nc


def _const_tables(sinks):
    slopes = np.exp2(-8.0 * np.arange(1, 17, dtype=np.float32) / 16.0).astype(np.float32)
    q = np.arange(128)[:, None]
    k = np.arange(256)[None, :]
    dist = (q - k + 128).astype(np.float32)
    valid = (dist >= 0) & (dist <= 128)
    btab = np.empty((128, 16, 257), np.float32)
    for e, h in enumerate(PERM_HEADS):
        btab[:, e, 0:256] = np.where(valid, -slopes[h] * dist, NEG)
        btab[:, e, 256] = sinks[h]
    r = np.arange(128)
    t = (r % 8)[:, None]
    kj = np.arange(136)[None, :]
    dist_s = (t + 128 - kj).astype(np.float32)
    valid_s = (dist_s >= 0) & (dist_s <= 128)
    btabs = np.empty((128, 137), np.float32)
    btabs[:, 0:136] = np.where(valid_s, -slopes[r // 8][:, None] * dist_s, NEG)
    btabs[:, 136] = sinks[r // 8]
    i = np.arange(2)[:, None]
    kk = np.arange(130)[None, :]
    dist_m = (128 + i - kk).astype(np.float32)
    valid_m = (dist_m >= 0) & (dist_m <= 128)
    bmini = np.empty((2, 16, 131), np.float32)
    for e, h in enumerate(PERM_HEADS):
        bmini[:, e, 0:130] = np.where(valid_m, -slopes[h] * dist_m, NEG)
        bmini[:, e, 130] = sinks[h]
    return btab.reshape(128, 16 * 257), btabs, bmini.reshape(2, 16 * 131)


_NC_CACHE = {}


def kernel(x_prompt, x_sample, cache_k_window, cache_v_window, state_conv, state_ffn_conv,
           g_attn_norm, w_in, attn_sinks, conv_w, g_out_attn, g_out_conv, w_out,
           g_ffn_norm, w_gate, w_up, ffn_conv_w, ffn_conv_b, w_down, g_final):
    f32 = np.float32
    x_prompt = np.asarray(x_prompt, f32)
    x_sample = np.asarray(x_sample, f32)
    featperm = np.concatenate([np.arange(h * 64, (h + 1) * 64) for h in PERM_HEADS])
    w_in0 = np.asarray(w_in, f32)[0]
    cols = []
    for c in range(8):
        for base in (1536, 2560, 3584):
            cols.append(np.arange(base + c * 128, base + (c + 1) * 128))
    cols.append(np.arange(1024, 1536))
    cols.append(featperm)
    win = np.ascontiguousarray(w_in0[:, np.concatenate(cols)])
    w_out0 = np.asarray(w_out, f32)[0]
    wout = np.ascontiguousarray(np.concatenate([w_out0[featperm], w_out0[1024:]], axis=0))
    wg = np.ascontiguousarray(np.asarray(w_gate, f32)[0])
    wu = np.ascontiguousarray(np.asarray(w_up, f32)[0])
    wd = np.ascontiguousarray(np.asarray(w_down, f32)[0])

    def fm(v, n):
        return np.asarray(v, f32).reshape(n, 128).T

    vecs = np.concatenate([
        fm(np.asarray(g_attn_norm)[0], 16), fm(np.asarray(g_ffn_norm)[0], 16),
        fm(np.asarray(g_out_attn, f32)[0][featperm], 8), fm(np.asarray(g_out_conv)[0], 8),
        np.concatenate([fm(np.asarray(conv_w)[0, j], 8) for j in range(3)], axis=1),
        np.concatenate([fm(np.asarray(ffn_conv_w)[0, j], NFF) for j in range(3)], axis=1),
        fm(np.asarray(ffn_conv_b)[0], NFF)], axis=1).astype(f32)
    assert vecs.shape == (128, V_N)
    vecs = np.ascontiguousarray(vecs)
    sinks = np.asarray(attn_sinks, f32)[0]
    sinkbc = np.ascontiguousarray(np.broadcast_to(sinks[PERM_HEADS][None, :], (128, 16)))
    sinkrows = np.ascontiguousarray(np.repeat(sinks, 8)[:, None])
    btab, btabs, bmini = _const_tables(sinks)
    idn = np.eye(128, dtype=f32)
    gfin = np.ascontiguousarray(np.asarray(g_final, f32))
    ck_all = np.asarray(cache_k_window, f32)[0].reshape(128, 128, 256)
    cv_all = np.asarray(cache_v_window, f32)[0].reshape(128, 128, 256)
    sc_all = np.asarray(state_conv, f32)[0]
    sf_all = np.asarray(state_ffn_conv, f32)[0]

    in_maps = []
    for c in range(8):
        b, half = c // 2, c % 2
        xin = np.zeros((NT, D), f32)
        if half == 1:
            xin[0:130] = x_prompt[b, 1024 - 130:1024]
        xin[130:1154] = x_prompt[b, half * 1024:(half + 1) * 1024]
        xin[1154:1282] = x_sample[16 * c:16 * (c + 1)].reshape(128, D)
        hm = np.full((128, 130), NEG if half == 0 else 0.0, f32)
        in_maps.append({
            "xin": xin,
            "ck": np.ascontiguousarray(ck_all[16 * c:16 * (c + 1)]),
            "cv": np.ascontiguousarray(cv_all[16 * c:16 * (c + 1)]),
            "sc": np.ascontiguousarray(sc_all[16 * c:16 * (c + 1)].reshape(32, 1024)),
            "sf": np.ascontiguousarray(sf_all[16 * c:16 * (c + 1)].reshape(32, DFF)),
            "win": win, "wout": wout, "wg": wg, "wu": wu, "wd": wd,
            "vecs": vecs, "sinkbc": sinkbc, "sinkrows": sinkrows,
            "btab": btab, "btabs": btabs, "bmini": bmini, "hmask": hm, "idn": idn, "gfin": gfin,
        })

    if "nc" not in _NC_CACHE:
        _NC_CACHE["nc"] = build_nc()
    nc = _NC_CACHE["nc"]
    res = run_bass_kernel_spmd(nc, in_maps, core_ids=list(range(8)))
    R = res.results

    y_prompt = np.empty((4, 2048, D), f32)
    y_sample = np.empty((128, 8, D), f32)
    kwp = np.empty((1, 4, 128, 4, 64), f32)
    vwp = np.empty((1, 4, 128, 4, 64), f32)
    cvp = np.empty((1, 4, 2, 1024), f32)
    ffp = np.empty((1, 4, 2, DFF), f32)
    kws = np.empty((1, 128, 128, 4, 64), f32)
    vws = np.empty((1, 128, 128, 4, 64), f32)
    cvs = np.empty((1, 128, 2, 1024), f32)
    ffs = np.empty((1, 128, 2, DFF), f32)
    for c in range(8):
        b, half = c // 2, c % 2
        r = R[c]
        y_prompt[b, half * 1024:(half + 1) * 1024] = r["yp"]
        y_sample[16 * c:16 * (c + 1)] = r["ys"].reshape(16, 8, D)
        if half == 1:
            kwp[0, b] = r["kwp"].reshape(128, 4, 64)
            vwp[0, b] = r["vwp"].reshape(128, 4, 64)
            cvp[0, b] = r["cvp"]
            ffp[0, b] = r["ffp"]
        kws[0, 16 * c:16 * (c + 1)] = r["kws"].reshape(16, 128, 4, 64)
        vws[0, 16 * c:16 * (c + 1)] = r["vws"].reshape(16, 128, 4, 64)
        cvs[0, 16 * c:16 * (c + 1)] = r["cvs"].reshape(16, 2, 1024)
        ffs[0, 16 * c:16 * (c + 1)] = r["ffs"].reshape(16, 2, DFF)
    return (y_prompt, y_sample, kwp, vwp, cvp, ffp, kws, vws, cvs, ffs)
```

```python
import numpy as np
from contextlib import ExitStack
import concourse.bass as bass
import concourse.mybir as mybir
from concourse.bass_utils import run_bass_kernel_spmd

F32 = mybir.dt.float32
BF16 = mybir.dt.bfloat16
ALU = mybir.AluOpType
AF = mybir.ActivationFunctionType
AX = mybir.AxisListType
_ESZ = {F32: 4, BF16: 2}

D = 2048
DFF = 5632
NFF = 44
NT = 1282
EPS = 1e-6
NEG = -1e30

PERM_HEADS = [4 * (2 * (qc // 4) + half) + (qc % 4) for qc in range(8) for half in range(2)]


class _Op:
    __slots__ = ("eng", "fn", "deps", "dmadeps", "sig", "semval", "dma", "dsem", "dval", "prev_same_sem")


class Tracker:
    CE = ("pe", "act", "dve", "pool")

    def __init__(self, nc, es, same_engine_sync=True, nq=12):
        self.nc = nc
        self.same = same_engine_sync
        self.fold = True
        self.eobj = {"pe": nc.tensor, "act": nc.scalar, "dve": nc.vector, "pool": nc.gpsimd, "sp": nc.sync}
        self.ops = []
        self.recs = {}
        self.dram = set()
        self.csem = {e: es.enter_context(nc.semaphore("s_" + e)) for e in self.CE}
        self.nq = nq
        self.qsem = {q: [es.enter_context(nc.semaphore(f"q_{q}{i}")) for i in range(nq)] for q in ("sp", "pool")}
        self.qcnt = {q: 0 for q in self.qsem}
        self.qlast = {q: [None] * nq for q in self.qsem}

    def region(self, ap):
        name = ap.tensor.name
        if name in self.dram:
            return None
        if name.startswith("psb"):
            return (name, 0, 128, 0, 2048)
        esz = _ESZ[ap.dtype]
        pat = ap.ap
        off = int(ap.offset)
        ps, pn = pat[0]
        if ps == 0:
            ps = 1 << 40
        p0 = off // ps
        f0 = off % ps
        ext = 0
        for st, cnt in pat[1:]:
            ext += (cnt - 1) * abs(st)
        return (name, p0, p0 + pn, f0 * esz, (f0 + ext + 1) * esz)

    def add(self, eng, fn, reads, writes, dma=None):
        op = _Op()
        op.eng = eng
        op.fn = fn
        op.dma = dma
        op.sig = False
        op.semval = None
        idx = len(self.ops)
        deps = {}
        dmadeps = set()
        ops = self.ops

        def dep_on(j):
            p = ops[j]
            if p.dma is not None:
                dmadeps.add(j)
            elif deps.get(p.eng, -1) < j:
                deps[p.eng] = j

        rregs = [r for r in (self.region(a) for a in reads if a is not None) if r is not None]
        wregs = [r for r in (self.region(a) for a in writes if a is not None) if r is not None]
        for (name, p0, p1, b0, b1) in rregs:
            psum = name.startswith("psb")
            for rec in self.recs.get(name, ()):
                if (rec[5] or (psum and rec[6] != eng)) and rec[0] < p1 and p0 < rec[1] and rec[2] < b1 and b0 < rec[3]:
                    dep_on(rec[4])
        for (name, p0, p1, b0, b1) in wregs:
            lst = self.recs.get(name, [])
            keep = []
            for rec in lst:
                if rec[0] < p1 and p0 < rec[1] and rec[2] < b1 and b0 < rec[3]:
                    dep_on(rec[4])
                    if rec[0] >= p0 and rec[1] <= p1 and rec[2] >= b0 and rec[3] <= b1:
                        continue
                keep.append(rec)
            keep.append([p0, p1, b0, b1, idx, True, eng])
            self.recs[name] = keep
        for (name, p0, p1, b0, b1) in rregs:
            lst = self.recs.setdefault(name, [])
            found = False
            if dma is None:
                for rec in lst:
                    if (not rec[5]) and rec[6] == eng and rec[0] == p0 and rec[1] == p1 and rec[2] == b0 and rec[3] == b1:
                        rec[4] = idx
                        found = True
                        break
            if not found:
                lst.append([p0, p1, b0, b1, idx, False, eng if dma is None else "dma"])
        fdeps = {}
        for e, j in deps.items():
            if e == eng and dma is None and (e == "pe" or not self.same):
                continue
            fdeps[e] = j
            ops[j].sig = True
        op.deps = fdeps
        op.dmadeps = dmadeps
        if dma is not None:
            q = dma
            c = self.qcnt[q]
            self.qcnt[q] = c + 1
            slot = c % self.nq
            op.dsem = self.qsem[q][slot]
            op.dval = 16 * (c // self.nq + 1)
            op.prev_same_sem = self.qlast[q][slot]
            self.qlast[q][slot] = idx
        ops.append(op)
        return idx

    def emit(self):
        cnt = {e: 0 for e in self.CE}
        for op in self.ops:
            if op.dma is None and op.sig:
                cnt[op.eng] += 1
                op.semval = cnt[op.eng]
        water = {}

        def wait(eng, sem, val):
            key = (eng, id(sem))
            if water.get(key, 0) >= val:
                return
            water[key] = val
            self.eobj[eng].wait_ge(sem, val)

        for op in self.ops:
            eng = op.eng
            need = []
            for e, j in op.deps.items():
                need.append((self.csem[e], self.ops[j].semval))
            for j in op.dmadeps:
                p = self.ops[j]
                need.append((p.dsem, p.dval))
            if op.dma is not None and op.prev_same_sem is not None:
                p = self.ops[op.prev_same_sem]
                need.append((p.dsem, p.dval))
            need = [(sm, v) for (sm, v) in need if water.get((eng, id(sm)), 0) < v]
            own = need.pop() if (need and op.dma is None and self.fold) else None
            for sm, v in need:
                wait(eng, sm, v)
            inst = op.fn()
            if own is not None:
                water[(eng, id(own[0]))] = own[1]
                inst._wait_ge(own[0], own[1])
            if op.dma is not None:
                inst.then_inc(op.dsem, 16)
            elif op.sig:
                inst.then_inc(self.csem[eng], 1)
        for q in self.qsem:
            for slot in range(self.nq):
                j = self.qlast[q][slot]
                if j is not None:
                    p = self.ops[j]
                    wait("sp", p.dsem, p.dval)
        return cnt

    def dma(self, q, out, in_, after=()):
        eo = self.eobj[q]
        return self.add(q, lambda: eo.dma_start(out=out, in_=in_), [in_, *after], [out], dma=q)

    def mm(self, out, lhsT, rhs, start=True, stop=True):
        nc = self.nc
        return self.add("pe", lambda: nc.tensor.matmul(out, lhsT, rhs, start=start, stop=stop), [lhsT, rhs], [out])

    def tr(self, out, in_, ident):
        nc = self.nc
        return self.add("pe", lambda: nc.tensor.transpose(out, in_, ident), [in_, ident], [out])

    def act(self, out, in_, func, bias=None, scale=1.0, accum_out=None):
        nc = self.nc
        kw = {}
        rd = [in_]
        if bias is not None:
            kw["bias"] = bias
            if not isinstance(bias, (int, float)):
                rd.append(bias)
        if not isinstance(scale, (int, float)):
            rd.append(scale)
        kw["scale"] = scale
        wr = [out]
        if accum_out is not None:
            kw["accum_out"] = accum_out
            wr.append(accum_out)
        return self.add("act", lambda: nc.scalar.activation(out=out, in_=in_, func=func, **kw), rd, wr)

    def tt(self, eng, out, in0, in1, op):
        eo = self.eobj[eng]
        return self.add(eng, lambda: eo.tensor_tensor(out=out, in0=in0, in1=in1, op=op), [in0, in1], [out])

    def ts(self, eng, out, in0, s1, s2, op0, op1=None):
        eo = self.eobj[eng]
        rd = [in0]
        if not isinstance(s1, (int, float)):
            rd.append(s1)
        if s2 is not None and not isinstance(s2, (int, float)):
            rd.append(s2)
        kw = {}
        if op1 is not None:
            kw["op1"] = op1
        return self.add(eng, lambda: eo.tensor_scalar(out=out, in0=in0, scalar1=s1, scalar2=s2, op0=op0, **kw), rd, [out])

    def stt(self, eng, out, in0, scalar, in1, op0, op1):
        eo = self.eobj[eng]
        rd = [in0, in1]
        if not isinstance(scalar, (int, float)):
            rd.append(scalar)
        return self.add(eng, lambda: eo.scalar_tensor_tensor(out=out, in0=in0, scalar=scalar, in1=in1, op0=op0, op1=op1), rd, [out])

    def copy(self, eng, out, in_):
        if eng == "act":
            nc = self.nc
            return self.add("act", lambda: nc.scalar.copy(out=out, in_=in_), [in_], [out])
        eo = self.eobj[eng]
        return self.add(eng, lambda: eo.tensor_copy(out=out, in_=in_), [in_], [out])

    def reduce(self, eng, out, in_, op):
        eo = self.eobj[eng]
        return self.add(eng, lambda: eo.tensor_reduce(out=out, in_=in_, axis=AX.X, op=op), [in_], [out])

    def recip(self, out, in_):
        nc = self.nc
        return self.add("dve", lambda: nc.vector.reciprocal(out=out, in_=in_), [in_], [out])

    def memset(self, eng, ap, val):
        eo = self.eobj[eng]
        return self.add(eng, lambda: eo.memset(ap, val), [], [ap])


V_GA, V_GF, V_GOA, V_GOC, V_CW, V_FW, V_FB, V_N = 0, 16, 32, 40, 48, 72, 204, 248

ARENA_BYTES = 212000
O_CONST = 0
O_JUNK = 4096
O_WR = 8192
O_A = 57344
O_B = 131072
O_C = 175008
assert O_C + 36992 <= ARENA_BYTES


def build_nc(same_engine_sync=True):
    nc = bass.Bass("TRN2", target_bir_lowering=False)
    with ExitStack() as es:
        T = Tracker(nc, es, same_engine_sync=same_engine_sync)

        def DI(name, shape):
            t = nc.dram_tensor(name, list(shape), F32, kind="ExternalInput")
            T.dram.add(t.name)
            return t.ap()

        def DO(name, shape):
            t = nc.dram_tensor(name, list(shape), F32, kind="ExternalOutput")
            T.dram.add(t.name)
            return t.ap()

        xin = DI("xin", (NT, D))
        ck_d = DI("ck", (16, 128, 256))
        cv_d = DI("cv", (16, 128, 256))
        sc_d = DI("sc", (32, 1024))
        sf_d = DI("sf", (32, DFF))
        win_d = DI("win", (D, 4608))
        wout_d = DI("wout", (D, D))
        wg_d = DI("wg", (D, DFF))
        wu_d = DI("wu", (D, DFF))
        wd_d = DI("wd", (DFF, D))
        vecs_d = DI("vecs", (128, V_N))
        sinkbc_d = DI("sinkbc", (128, 16))
        sinkrows_d = DI("sinkrows", (128, 1))
        btab_d = DI("btab", (128, 16 * 257))
        btabs_d = DI("btabs", (128, 137))
        bmini_d = DI("bmini", (2, 16 * 131))
        hmask_d = DI("hmask", (128, 130))
        idn_d = DI("idn", (128, 128))
        gfin_d = DI("gfin", (D,))

        yp_d = DO("yp", (1024, D))
        ys_d = DO("ys", (128, D))
        kwp_d = DO("kwp", (128, 256))
        vwp_d = DO("vwp", (128, 256))
        cvp_d = DO("cvp", (2, 1024))
        ffp_d = DO("ffp", (2, DFF))
        kws_d = DO("kws", (16, 128, 256))
        vws_d = DO("vws", (16, 128, 256))
        cvs_d = DO("cvs", (32, 1024))
        ffs_d = DO("ffs", (32, DFF))

        arena = es.enter_context(nc.sbuf_tensor("arena", [128, ARENA_BYTES // 4], F32))
        psb = [es.enter_context(nc.psum_tensor(f"psb{i}", [128, 512], F32)) for i in range(8)]

        def V(off, shape, dt=F32):
            n = 1
            for s in shape:
                n *= s
            nb = n * _ESZ[dt]
            assert off % 4 == 0 and nb % 4 == 0 and off + nb <= ARENA_BYTES, (off, shape)
            a = arena[:, off // 4:(off + nb) // 4]
            if dt != F32:
                a = a.bitcast(dt)
            if len(shape) == 2:
                a = a.rearrange("p (a b) -> p a b", a=shape[0])
            elif len(shape) == 3:
                a = a.rearrange("p (a b c) -> p a b c", a=shape[0], b=shape[1])
            return a

        def PSB(i):
            return psb[i][:, :].bitcast(BF16)

        o = O_CONST
        ident = V(o, [128]); o += 512
        identb = V(o, [128], BF16); o += 256
        onesf = V(o, [128]); o += 512
        vecs = V(o, [V_N]); o += V_N * 4
        sinkbc = V(o, [16]); o += 64
        sinkrows = V(o, [1]); o += 4
        epsc = V(o, [1]); o += 4
        hmask = V(o, [130]); o += 520
        xTm = V(o, [32]); o += 128
        rinv = V(o, [16]); o += 64
        NSC = 192
        stats = V(o, [NSC]); o += NSC * 4
        ssq2 = V(o, [36]); o += 144
        assert o <= O_JUNK, o
        junk = V(O_JUNK, [2048], BF16)
        scnt = [0]

        def scol():
            i = scnt[0] % NSC
            scnt[0] += 1
            return stats[:, i:i + 1]

        gA = vecs[:, V_GA:V_GA + 16]
        gF = vecs[:, V_GF:V_GF + 16]
        gOA = vecs[:, V_GOA:V_GOA + 8]
        gOC = vecs[:, V_GOC:V_GOC + 8]

        def cw(j, c):
            return vecs[:, V_CW + j * 8 + c:V_CW + j * 8 + c + 1]

        def fw(j, f):
            return vecs[:, V_FW + j * NFF + f:V_FW + j * NFF + f + 1]

        def fb(f):
            return vecs[:, V_FB + f:V_FB + f + 1]

        T.dma("sp", ident, idn_d)
        T.dma("pool", identb, idn_d)
        T.dma("sp", vecs, vecs_d)
        T.dma("sp", sinkbc, sinkbc_d)
        T.dma("sp", sinkrows, sinkrows_d)
        T.dma("sp", hmask, hmask_d)
        T.memset("dve", onesf, 1.0)
        T.memset("dve", epsc, EPS)

        wspecs = []

        def wv16(i, cols):
            return V(O_WR + (i % 3) * 16384, [16, cols], BF16)

        def wv8(i, shape):
            return V(O_WR + (i % 6) * 8192, shape, BF16)

        win_v = win_d.rearrange("(k p) c -> p k c", p=128)
        wout_v = wout_d.rearrange("(k p) c -> p k c", p=128)
        wg_v = wg_d.rearrange("(k p) c -> p k c", p=128)
        wu_v = wu_d.rearrange("(k p) c -> p k c", p=128)
        wd_v = wd_d.rearrange("(c p) n -> p c n", p=128)
        li = 0
        for c in range(8):
            wspecs.append((win_v[:, :, c * 384:(c + 1) * 384], wv16(li, 384))); li += 1
        wspecs.append((win_v[:, :, 3072:3584], wv16(li, 512))); li += 1
        for h in range(2):
            wspecs.append((win_v[:, :, 3584 + h * 512:3584 + (h + 1) * 512], wv16(li, 512))); li += 1
        for nb in range(4):
            wspecs.append((wout_v[:, :, nb * 512:(nb + 1) * 512], wv16(li, 512))); li += 1
        N16 = li
        fi_ = 0
        def _dspecs(grp):
            nonlocal fi_
            for hg in range(2):
                f0 = grp * 4 + hg * 2
                wspecs.append((wd_v[:, f0:f0 + 2, :], wv8(fi_, [2, 2048]))); fi_ += 1

        for grp in range(11):
            for hg in range(2):
                c0 = (grp * 4 + hg * 2) * 128
                wspecs.append((wg_v[:, :, c0:c0 + 256], wv8(fi_, [16, 256]))); fi_ += 1
                wspecs.append((wu_v[:, :, c0:c0 + 256], wv8(fi_, [16, 256]))); fi_ += 1
            if grp >= 1:
                _dspecs(grp - 1)
        _dspecs(10)
        wstate = {"issued": 0, "cons": 0}

        def wget(second=False):
            i = wstate["cons"]
            wstate["cons"] += 1
            floor = i - 1 if second else i
            def sz(n):
                return 16384 if n < N16 else 8192
            while wstate["issued"] < len(wspecs):
                n = wstate["issued"]
                if n > i and sum(sz(m) for m in range(floor, n + 1)) > 49152:
                    break
                src, dst = wspecs[n]
                T.dma("pool", dst, src, after=([xstage[1][:, :]] if n in (1, 2) else ()))
                wstate["issued"] += 1
            return wspecs[i][1]

        hT = V(O_B, [16, NT], BF16)
        xstage = [V(O_C + i * 8192, [2048]) for i in range(4)]
        mixedT = V(O_C, [16, 1156], BF16)
        sconv = V(O_A, [8, 1156])
        rstd_bc = V(O_A + 36992, [1156])
        sqb = [V(O_A + 41616 + i * 2048, [512]) for i in range(2)]
        cu_keep = V(O_A + 45712, [8, 34])
        convout = V(O_A + 46800, [1024])
        eA = O_A + 54648
        cu = V(eA, [NT])
        tmpU = [V(eA + 5128 + i * 2048, [512]) for i in range(2)]
        tmpA = [V(eA + 9224 + i * 2048, [512]) for i in range(2)]
        scT = V(eA + 13320, [8, 32])
        scstage = V(eA + 14344, [1024])
        cus_ext = V(eA + 18440, [16, 10])
        assert eA + 18440 + 640 <= O_A + 73728
        Vc = V(O_A, [16, 256], BF16)
        KTs = V(O_A + 8192, [2, 16, 136], BF16)
        Vnew = V(O_A + 16896, [16, 256], BF16)
        QT = V(O_A + 25088, [8, NT], BF16)
        QSpad = V(O_A + 45600, [2, 16, 128], BF16)
        assert O_A + 45600 + 8192 <= eA
        KT = V(eA, [2, NT], BF16)
        vtm = V(eA + 5128, [11, 256], BF16)
        kc_tm = V(eA + 10760, [16, 256], BF16)
        assert eA + 10760 + 8192 <= O_A + 73728
        kvstage = [V(O_JUNK + i * 2048, [512]) for i in range(2)]
        btab = V(O_B, [16, 257])
        btabs = V(O_B + 16448, [137])
        bmini = V(O_B + 16448 + 548, [16, 131])
        ob = O_B + 16448 + 548 + 8384
        S_sb = [V(ob + i * 1032, [258]) for i in range(2)]; ob += 2064
        P_sb = [V(ob + i * 520, [260], BF16) for i in range(2)]; ob += 1040
        Pf_sb = [V(ob + i * 552, [138]) for i in range(2)]; ob += 1104
        PT_sb = [V(ob + i * 512, [256], BF16) for i in range(2)]; ob += 1024
        o_attn_sb = ob
        attn_sb = V(ob, [1024]); ob += 4096
        attn_sc = V(ob, [1024]); ob += 4096
        rb_s = V(ob, [128]); ob += 512
        rs_all = [V(ob + i * 64, [16]) for i in range(2)]; ob += 128
        ng_all = [V(ob + i * 64, [16]) for i in range(2)]; ob += 128
        assert ob <= O_B + 41024, ob
        attnT_s = V(eA + 10760, [8, 128])
        sq_s = V(eA + 10760 + 4096, [1024])
        x2acc = V(O_A, [9, 2048])
        h2T = V(O_B, [16, 1154], BF16)
        xres = [V(O_JUNK + i * 2048, [512]) for i in range(2)]
        xs2b = [V(O_C + i * 8192, [2048]) for i in range(2)]
        x2Tm = V(O_B + 36928, [32])
        sqm = V(O_B + 36928 + 128, [32])
        rm = V(O_B + 36928 + 256, [2])
        actb = [V(O_C + i * 9216, [4, 1152], BF16) for i in range(2)]
        g_sb = V(O_C + 18432, [NT])
        tAb = [V(O_C + 23560 + i * 2048, [512]) for i in range(2)]
        tSb = V(O_C + 27656, [512])
        sfT = V(O_C + 29704, [NFF, 32])
        gs_ext = V(O_C + 35336, [16, 10])
        assert O_C + 35336 + 640 <= ARENA_BYTES
        ffn_keep = V(O_B + 36928 + 512, [NFF, 34])
        assert O_B + 36928 + 512 + NFF * 34 * 4 <= O_C
        sfstage = V(O_C, [1408])
        ffstage = V(O_B + 16384, [2048])

        def bc_mid(ap, n):
            return ap.unsqueeze(2).to_broadcast([ap.shape[0], ap.shape[1], n])

        def rstd_of(ssq_ap, rows, scale):
            a = scol()
            b = scol()
            T.act(a[0:rows], ssq_ap, AF.Sqrt, bias=epsc[0:rows], scale=scale)
            T.recip(b[0:rows], a[0:rows])
            return b

        tilesA = [(0, 2)] + [(2 + 128 * i, 130 + 128 * i) for i in range(9)] + [(1154, 1282)]
        T.dma("sp", xstage[3][0:2, :], xin[128:130, :])
        for k in range(16):
            T.tr(psb[7][:, 2 * k:2 * k + 2], xstage[3][0:2, k * 128:(k + 1) * 128], ident[0:2, 0:2])
        T.copy("dve", xTm, psb[7][:, 0:32])
        def norm1_tile(i):
            c0, c1 = tilesA[i]
            rows = c1 - c0
            xs_ = xstage[i % 4]
            T.dma("sp", xs_[0:rows, :], xin[c0:c1, :])
            ssq = scol()
            T.act(junk[0:rows, :], xs_[0:rows, :], AF.Square, accum_out=ssq[0:rows])
            rs = rstd_of(ssq[0:rows], rows, 1.0 / D)
            T.act(xs_[0:rows, :], xs_[0:rows, :], AF.Copy, scale=rs[0:rows])
            for kb in range(4):
                bank = (i % 2) * 4 + kb
                for j in range(4):
                    k = kb * 4 + j
                    T.tr(psb[bank][:, j * 128:j * 128 + rows], xs_[0:rows, k * 128:(k + 1) * 128], ident[0:rows, 0:rows])
                T.tt("dve", hT[:, kb * 4:kb * 4 + 4, c0:c1],
                     psb[bank][:, :].rearrange("p (a b) -> p a b", a=4)[:, :, 0:rows],
                     bc_mid(gA[:, kb * 4:kb * 4 + 4], rows), ALU.mult)

        for i in range(6):
            norm1_tile(i)
        T.dma("sp", kws_d[:, 0:120, :], ck_d[:, 8:128, :])
        T.dma("sp", vws_d[:, 0:120, :], cv_d[:, 8:128, :])
        T.dma("sp", scstage[0:32, :], sc_d)
        for c in range(8):
            T.tr(psb[6][:, c * 32:(c + 1) * 32], scstage[0:32, c * 128:(c + 1) * 128], ident[0:32, 0:32])
        T.copy("dve", scT.rearrange("p a b -> p (a b)"), psb[6][:, 0:256])
        NTQ = [(126, 638), (638, 1150), (1150, 1282)]
        it = 0
        for c in range(8):
            slot = wget()
            for ti, (n0, n1) in enumerate(NTQ):
                if c == 0 and ti == 1:
                    for i in range(6, 10):
                        norm1_tile(i)
                if c == 0 and ti == 2:
                    norm1_tile(10)
                W = n1 - n0
                pB, pC, pU = psb[(3 * it) % 8], psb[(3 * it + 1) % 8], psb[(3 * it + 2) % 8]
                for wi, pX in enumerate((pB, pC, pU)):
                    for k in range(16):
                        T.mm(pX[:, 0:W], slot[:, k, wi * 128:(wi + 1) * 128], hT[:, k, n0:n1], start=(k == 0), stop=(k == 15))
                tU = tmpU[it % 2][:, 0:W]
                T.copy("act", tU, pU[:, 0:W])
                T.tt("dve", cu[:, n0:n1], pC[:, 0:W], tU, ALU.mult)
                lo, hi = max(n0, 128), min(n1, 1154)
                if hi > lo:
                    L = hi - lo
                    tA = tmpA[it % 2][:, 0:L]
                    T.act(tA, cu[:, lo - 2:hi - 2], AF.Copy, scale=cw(0, c))
                    T.stt("dve", tA, cu[:, lo - 1:hi - 1], cw(1, c), tA, ALU.mult, ALU.add)
                    T.stt("dve", tA, cu[:, lo:hi], cw(2, c), tA, ALU.mult, ALU.add)
                    T.tt("dve", sconv[:, c, lo - 126:hi - 126], pB[:, lo - n0:hi - n0], tA, ALU.mult)
                if ti == 2:
                    T.copy("dve", cus_ext[:, :, 2:10], cu[:, 1154:1282].rearrange("p (s t) -> p s t", s=16))
                    T.copy("act", cus_ext[:, :, 0:2], scT[:, c, :].rearrange("p (s r) -> p s r", s=16))
                    tA3 = tmpA[(it + 1) % 2][:, 0:128].rearrange("p (s t) -> p s t", s=16)
                    T.act(tA3, cus_ext[:, :, 0:8], AF.Copy, scale=cw(0, c))
                    T.stt("dve", tA3, cus_ext[:, :, 1:9], cw(1, c), tA3, ALU.mult, ALU.add)
                    T.stt("dve", tA3, cus_ext[:, :, 2:10], cw(2, c), tA3, ALU.mult, ALU.add)
                    T.tt("dve", sconv[:, c, 1028:1156].rearrange("p (s t) -> p s t", s=16),
                         pB[:, 4:132].rearrange("p (s t) -> p s t", s=16), tA3, ALU.mult)
                    T.copy("act", cu_keep[:, c, 0:2], cu[:, 1152:1154])
                    T.copy("act", cu_keep[:, c, 2:34].rearrange("p (s r) -> p s r", s=16), cus_ext[:, :, 8:10])
                it += 1
        MT = [(2, 514), (514, 1026), (1026, 1156)]
        it = 0
        for c in range(8):
            for ti, (m0, m1) in enumerate(MT):
                sq = sqb[it % 2][:, 0:m1 - m0]
                T.act(sq, sconv[:, c, m0:m1], AF.Square)
                T.mm(psb[ti][:, 0:m1 - m0], onesf, sq, start=(c == 0), stop=(c == 7))
                it += 1
        for ti, (m0, m1) in enumerate(MT):
            T.act(rstd_bc[:, m0:m1], psb[ti][:, 0:m1 - m0], AF.Sqrt, bias=epsc, scale=1.0 / 1024)
            T.recip(rstd_bc[:, m0:m1], rstd_bc[:, m0:m1])
        for c in range(8):
            T.stt("dve", mixedT[:, 8 + c, 2:1156], sconv[:, c, 2:1156], gOC[:, c:c + 1],
                  rstd_bc[:, 2:1156], ALU.mult, ALU.mult)
        for c in range(8):
            T.tr(psb[6 + c // 4][0:34, (c % 4) * 128:(c % 4 + 1) * 128], cu_keep[:, c, :], ident)
        T.copy("act", convout[0:34, 0:512], psb[6][0:34, :])
        T.copy("act", convout[0:34, 512:1024], psb[7][0:34, :])
        T.dma("sp", cvp_d, convout[0:2, :])
        T.dma("sp", cvs_d, convout[2:34, :])

        slot = wget()
        T.dma("pool", kc_tm, ck_d.rearrange("s k d -> k s d"))
        NTK = [(0, 512), (512, 1024), (1024, 1282)]
        it = 0
        for pc in range(2):
            for ti, (n0, n1) in enumerate(NTK):
                W = n1 - n0
                ps = psb[it % 4]
                for k in range(16):
                    T.mm(ps[:, 0:W], slot[:, k, pc * 128:(pc + 1) * 128], hT[:, k, n0:n1], start=(k == 0), stop=(k == 15))
                T.copy("act" if it % 2 == 0 else "dve", KT[:, pc, n0:n1], ps[:, 0:W])
                it += 1
        for i, (c0, c1) in enumerate(tilesA):
            rows = c1 - c0
            full = i in (9, 10)
            ps = psb[4 + i % 2]
            r0 = 0 if full else 256
            N = 512 - r0
            for k in range(16):
                T.mm(ps[0:rows, 0:N], hT[:, k, c0:c1], slot[:, k, r0:512], start=(k == 0), stop=(k == 15))
            if full:
                stg = kvstage[i % 2]
                T.copy("dve", stg, ps[:, 0:512])
                T.copy("act", vtm[:, i, :], ps[:, 256:512])
                if i == 9:
                    T.dma("sp", kwp_d, stg[:, 0:256])
                    T.dma("sp", vwp_d, stg[:, 256:512])
                else:
                    for s in range(16):
                        T.dma("sp", kws_d[s, 120:128, :], stg[s * 8:(s + 1) * 8, 0:256])
                        T.dma("sp", vws_d[s, 120:128, :], stg[s * 8:(s + 1) * 8, 256:512])
            else:
                T.copy("act" if i % 2 == 0 else "dve", vtm[0:rows, i, :], ps[0:rows, 0:256])
        for sg in range(4):
            pv = PSB(6 + sg % 2)
            for si in range(4):
                s = sg * 4 + si
                for pr in range(2):
                    T.tr(pv[:, (si * 2 + pr) * 128:(si * 2 + pr + 1) * 128], kc_tm[:, s, pr * 128:(pr + 1) * 128], identb)
            T.copy("dve" if sg % 2 == 0 else "act",
                   KTs[:, :, sg * 4:(sg + 1) * 4, 0:128].rearrange("p pr s k -> p s pr k"),
                   pv.rearrange("p (s pr k) -> p s pr k", s=4, pr=2))
        for pr in range(2):
            T.copy("dve", KTs[:, pr, :, 128:136], KT[:, pr, 1154:1282].rearrange("p (s t) -> p s t", s=16))
        for s in range(16):
            T.dma("sp", Vnew[0:8, s, :], vtm[s * 8:(s + 1) * 8, 10, :])

        T.memset("pool", QSpad.rearrange("p a b c -> p (a b c)"), 0.0)
        it = 0
        for hl in range(2):
            slot = wget()
            for jj in range(4):
                qc = hl * 4 + jj
                for ti, (n0, n1) in enumerate(NTQ):
                    W = n1 - n0
                    ps = psb[it % 4]
                    for k in range(16):
                        T.mm(ps[:, 0:W], slot[:, k, jj * 128:(jj + 1) * 128], hT[:, k, n0:n1], start=(k == 0), stop=(k == 15))
                    T.copy("act" if it % 2 == 0 else "dve", QT[:, qc, n0:n1], ps[:, 0:W])
                    if ti == 2:
                        for half in range(2):
                            j = 2 * (qc // 4) + half
                            g = qc % 4
                            T.copy("dve", QSpad[half * 64:(half + 1) * 64, qc // 4, :, 32 * j + 8 * g:32 * j + 8 * g + 8],
                                   ps[half * 64:(half + 1) * 64, 4:132].rearrange("p (s t) -> p s t", s=16))
                    it += 1
        T.dma("pool", Vc, cv_d.rearrange("s k d -> k s d"))
        T.dma("sp", bmini[0:2].rearrange("p a b -> p (a b)"), bmini_d)
        btab_dv = btab_d.rearrange("p (a b) -> p a b", a=16)
        for hg in range(4):
            T.dma("sp", btab[:, hg * 4:(hg + 1) * 4, :], btab_dv[:, hg * 4:(hg + 1) * 4, :])
        T.dma("sp", btabs, btabs_d)

        T.memset("dve", psb[0][:, :], 0.0)
        T.memset("dve", psb[1][:, :], 0.0)
        blocks = []
        blocks.append(dict(rows=2, q0=128, k0=0, nk=130, segs=[(0, 2, vtm[:, 0, :]), (2, 130, vtm[:, 1, :])],
                           bias=lambda e: bmini[0:2, e, :], mask=hmask[0:2, 0:130], mcol=2))
        for n in range(1, 9):
            q0 = 130 + 128 * (n - 1)
            blocks.append(dict(rows=128, q0=q0, k0=q0 - 128, nk=256,
                               segs=[(0, 128, vtm[:, n, :]), (128, 256, vtm[:, n + 1, :])],
                               bias=lambda e: btab[:, e, :], mask=(hmask[:, 0:128] if n == 1 else None), mcol=q0 - 126))
        units = [(bi, e) for bi in range(len(blocks)) for e in range(16)]
        NU = len(units)

        def st_A(u):
            bi, e = units[u]
            b_ = blocks[bi]
            rows, nk = b_["rows"], b_["nk"]
            qc, half = e // 2, e % 2
            pb = half * 64
            T.mm(psb[u % 2][0:rows, 0:nk], QT[pb:pb + 64, qc, b_["q0"]:b_["q0"] + rows],
                 KT[pb:pb + 64, qc // 4, b_["k0"]:b_["k0"] + nk])

        def st_B(u):
            bi, e = units[u]
            b_ = blocks[bi]
            rows, nk = b_["rows"], b_["nk"]
            Ssb = S_sb[u % 2][0:rows, 0:nk + 1]
            T.stt("dve", Ssb, psb[u % 2][0:rows, 0:nk + 1], 0.125, b_["bias"](e), ALU.mult, ALU.add)
            if b_["mask"] is not None:
                mk = b_["mask"].shape[1]
                T.tt("dve", Ssb[:, 0:mk], Ssb[:, 0:mk], b_["mask"], ALU.add)
            T.add("dve", lambda o=ng_all[bi % 2][0:rows, e:e + 1], i=Ssb: nc.vector.tensor_reduce(
                out=o, in_=i, axis=AX.X, op=ALU.max, negate=True), [Ssb], [ng_all[bi % 2][0:rows, e:e + 1]])

        def st_C(u):
            bi, e = units[u]
            b_ = blocks[bi]
            rows, nk = b_["rows"], b_["nk"]
            T.act(P_sb[u % 2][0:rows, 0:nk + 1], S_sb[u % 2][0:rows, 0:nk + 1], AF.Exp, bias=ng_all[bi % 2][0:rows, e:e + 1],
                  accum_out=rs_all[bi % 2][0:rows, e:e + 1])

        def st_E(u):
            bi, e = units[u]
            b_ = blocks[bi]
            rows = b_["rows"]
            PTp = PSB(2 + u % 2)
            for si, (k0, k1, Vap) in enumerate(b_["segs"]):
                T.tr(PTp[0:k1 - k0, si * 128:si * 128 + rows], P_sb[u % 2][0:rows, k0:k1], identb[0:rows, 0:rows])

        def st_F(u):
            bi, e = units[u]
            b_ = blocks[bi]
            rows = b_["rows"]
            PTp = PSB(2 + u % 2)
            if rows == 128 and all(k1 - k0 == 128 for (k0, k1, _) in b_["segs"]):
                ns = len(b_["segs"])
                T.copy("act", PT_sb[u % 2][:, 0:128 * ns], PTp[:, 0:128 * ns])
                return
            for si, (k0, k1, Vap) in enumerate(b_["segs"]):
                T.copy("act", PT_sb[u % 2][0:k1 - k0, si * 128:si * 128 + rows],
                       PTp[0:k1 - k0, si * 128:si * 128 + rows])

        def st_G(u):
            bi, e = units[u]
            b_ = blocks[bi]
            rows = b_["rows"]
            j = 2 * ((e // 2) // 4) + e % 2
            ops_ = psb[4 + 2 * (bi % 2) + e // 8][0:rows, (e % 8) * 64:(e % 8 + 1) * 64]
            ns = len(b_["segs"])
            for si, (k0, k1, Vap) in enumerate(b_["segs"]):
                T.mm(ops_, PT_sb[u % 2][0:k1 - k0, si * 128:si * 128 + rows], Vap[0:k1 - k0, j * 64:(j + 1) * 64],
                     start=(si == 0), stop=(si == ns - 1))

        p1st = {}

        def post1a(bi):
            rows = blocks[bi]["rows"]
            T.recip(rinv[0:rows], rs_all[bi % 2][0:rows])
            A_sb = attn_sb[0:rows]
            for bnk in range(2):
                T.tt("dve", A_sb[:, bnk * 512:(bnk + 1) * 512].rearrange("p (h d) -> p h d", h=8),
                     psb[4 + 2 * (bi % 2) + bnk][0:rows, :].rearrange("p (h d) -> p h d", h=8),
                     bc_mid(rinv[0:rows, bnk * 8:(bnk + 1) * 8], 64), ALU.mult)

        def post1b(bi):
            rows = blocks[bi]["rows"]
            ssq = scol()
            T.act(junk[0:rows, 0:1024], attn_sb[0:rows], AF.Square, accum_out=ssq[0:rows])
            p1st[bi] = ssq

        def post1c(bi):
            rows = blocks[bi]["rows"]
            a = scol()
            T.act(a[0:rows], p1st[bi][0:rows], AF.Sqrt, bias=epsc[0:rows], scale=1.0 / 1024)
            p1st[bi] = a

        def post1d(bi):
            rows = blocks[bi]["rows"]
            b = scol()
            T.recip(b[0:rows], p1st[bi][0:rows])
            T.tt("pool", attn_sc[0:rows], attn_sb[0:rows], b[0:rows, 0:1].to_broadcast([rows, 1024]), ALU.mult)

        def post2(bi):
            b_ = blocks[bi]
            rows, mcol = b_["rows"], b_["mcol"]
            for bnk in range(2):
                pbk = psb[4 + 2 * (bi % 2) + bnk]
                for jj in range(4):
                    c = bnk * 4 + jj
                    T.tr(pbk[:, jj * 128:jj * 128 + rows], attn_sc[0:rows, c * 128:(c + 1) * 128], ident[0:rows, 0:rows])
                T.tt("dve", mixedT[:, bnk * 4:bnk * 4 + 4, mcol:mcol + rows],
                     pbk[:, :].rearrange("p (a b) -> p a b", a=4)[:, :, 0:rows],
                     bc_mid(gOA[:, bnk * 4:bnk * 4 + 4], rows), ALU.mult)

        NSU = 16
        sden = {}

        def ss_A(s):
            for pr in range(2):
                T.mm(psb[s % 2][:, 300:436], QSpad[:, pr, s, :], KTs[:, pr, s, :], start=(pr == 0), stop=(pr == 1))

        def ss_B(s):
            Ssb = S_sb[s % 2][:, 0:137]
            T.stt("dve", Ssb, psb[s % 2][:, 300:437], 0.125, btabs, ALU.mult, ALU.add)
            negm = scol()
            T.add("dve", lambda o=negm, i=Ssb: nc.vector.tensor_reduce(out=o, in_=i, axis=AX.X, op=ALU.max, negate=True),
                  [Ssb], [negm])
            sden[s] = negm

        def ss_C(s):
            negm = sden[s]
            rsum = scol()
            T.act(Pf_sb[s % 2][:, 0:137], S_sb[s % 2][:, 0:137], AF.Exp, bias=negm, accum_out=rsum)
            sden[s] = rsum

        def ss_D(s):
            rv = scol()
            T.recip(rv, sden[s])
            T.ts("dve", P_sb[s % 2][:, 0:136], Pf_sb[s % 2][:, 0:136], rv, None, ALU.mult)

        def ss_E(s):
            PTp = PSB(2 + s % 2)
            Pn = P_sb[s % 2][:, 0:136]
            T.tr(PTp[:, 0:128], Pn[:, 0:128], identb)
            T.tr(PTp[0:8, 128:256], Pn[:, 128:136], identb)

        def ss_F(s):
            PTp = PSB(2 + s % 2)
            T.copy("act", PT_sb[s % 2][:, 0:128], PTp[:, 0:128])
            T.copy("act", PT_sb[s % 2][0:8, 128:256], PTp[0:8, 128:256])

        def ss_G(s):
            Ops = psb[sbank0 + s % 2]
            PTs = PT_sb[s % 2]
            for pr in range(2):
                T.mm(Ops[:, pr * 64:(pr + 1) * 64], Vc[:, s, pr * 128:(pr + 1) * 128], PTs[:, pr * 64:(pr + 1) * 64],
                     start=True, stop=False)
                T.mm(Ops[:, pr * 64:(pr + 1) * 64], Vnew[0:8, s, pr * 128:(pr + 1) * 128],
                     PTs[0:8, 128 + pr * 64:128 + (pr + 1) * 64], start=False, stop=True)

        def ss_H(s):
            Ops = psb[sbank0 + s % 2]
            for half in range(2):
                src = Ops[half * 64:(half + 1) * 64, 0:128].rearrange("p (pr h g t) -> p pr h g t", pr=2, h=2, g=4)[:, :, half, :, :]
                dst = attnT_s[half * 64:(half + 1) * 64, :, 8 * s:8 * s + 8].rearrange("p (pr g) t -> p pr g t", pr=2)
                T.copy("dve", dst, src)

        sbank0 = 4 + 2 * (1 - (len(blocks) - 1) % 2) if blocks else 4

        def sample_iter(t):
            if t == 0 and NSU:
                ss_A(0)
            if t + 1 < NSU:
                ss_A(t + 1)
            if t < NSU:
                ss_B(t)
                ss_C(t)
            if 0 <= t - 1 < NSU:
                ss_D(t - 1)
                ss_E(t - 1)
                ss_F(t - 1)
            if 0 <= t - 2 < NSU:
                ss_G(t - 2)
            if 0 <= t - 3 < NSU:
                ss_H(t - 3)

        pend = {}
        if NU:
            st_A(0)
        for t in range(NU + 10):
            if t + 1 < NU:
                st_A(t + 1)
            if t < NU:
                st_B(t)
                st_C(t)
            if 0 <= t - 1 < NU:
                st_E(t - 1)
                st_F(t - 1)
            if 0 <= t - 2 < NU:
                st_G(t - 2)
                bi, e = units[t - 2]
                if e == 15:
                    pend.setdefault(t, []).append((post1a, bi))
                    pend.setdefault(t + 1, []).append((post1b, bi))
                    pend.setdefault(t + 2, []).append((post1c, bi))
                    pend.setdefault(t + 3, []).append((post1d, bi))
                    pend.setdefault(t + 6, []).append((post2, bi))
            for fn_, bi_ in pend.pop(t, []):
                fn_(bi_)
            if t >= NU:
                sample_iter(t - NU)
        assert not pend
        for t in range(10, NSU + 4):
            sample_iter(t)
        T.act(sq_s, attnT_s.rearrange("p a b -> p (a b)"), AF.Square)
        for c in range(8):
            T.mm(psb[6][:, 0:128], onesf, sq_s[:, c * 128:(c + 1) * 128], start=(c == 0), stop=(c == 7))
        T.act(rb_s, psb[6][:, 0:128], AF.Sqrt, bias=epsc, scale=1.0 / 1024)
        T.recip(rb_s, rb_s)
        for c in range(8):
            T.stt("dve", mixedT[:, c, 1028:1156], attnT_s[:, c, :], gOA[:, c:c + 1], rb_s, ALU.mult, ALU.mult)

        rt = [(130 + 128 * i, 258 + 128 * i) for i in range(8)] + [(1154, 1282)]
        dgb = [V(O_B + 37440 + i * 512, [128]) for i in range(2)]
        rbcb = [V(O_B + 37440 + 1024 + i * 512, [128]) for i in range(2)]
        dummy6 = V(O_B + 37440 + 2048, [512], BF16)
        n2 = {"g": 0, "e": 0, "slot": 0}
        n2q = []
        evt = [V(O_B + 37440 + 3072 + i * 512, [128]) for i in range(4)]

        n2rs = {}

        def norm2_stats(r):
            ssq = scol()
            T.reduce("dve", ssq, ssq2[:, 4 * r:4 * r + 4], ALU.add)
            rs = rstd_of(ssq, 128, 1.0 / D)
            T.ts("dve", dgb[r % 2], ident, rs, None, ALU.mult)

        def norm2_grp(r, kb):
            c0, c1 = rt[r]
            rb = rbcb[r % 2]
            if kb == 0:
                T.mm(psb[6][:, 0:128], onesf, dgb[r % 2])
                T.copy("act", rb, psb[6][:, 0:128])
            bank = 3 + n2["g"] % 3
            n2["g"] += 1
            for j in range(4):
                k = kb * 4 + j
                T.tr(psb[bank][:, j * 128:(j + 1) * 128], x2acc[:, r, k * 128:(k + 1) * 128], ident)
            for j in range(4):
                k = kb * 4 + j
                if kb % 2 == 0:
                    T.stt("dve", h2T[:, k, c0 - 128:c1 - 128], psb[bank][:, j * 128:(j + 1) * 128], gF[:, k:k + 1], rb,
                          ALU.mult, ALU.mult)
                else:
                    tmp = evt[n2["e"] % 4]
                    n2["e"] += 1
                    T.act(tmp, psb[bank][:, j * 128:(j + 1) * 128], AF.Copy, scale=gF[:, k:k + 1])
                    T.tt("pool", h2T[:, k, c0 - 128:c1 - 128], tmp, rb, ALU.mult)

        it = 0
        for nb in range(4):
            slot = wget()
            for jj in range(4):
                oc = nb * 4 + jj
                for k in range(16):
                    T.mm(psb[7][:, oc * 2:oc * 2 + 2], slot[:, k, jj * 128:(jj + 1) * 128], mixedT[:, k, 2:4],
                         start=(k == 0), stop=(k == 15))
            for r, (c0, c1) in enumerate(rt):
                ps = psb[it % 3]
                xr = xres[it % 2]
                T.dma("sp", xr, xin[c0:c1, nb * 512:(nb + 1) * 512])
                for k in range(16):
                    T.mm(ps[:, :], mixedT[:, k, c0 - 126:c1 - 126], slot[:, k, :], start=(k == 0), stop=(k == 15))
                    if nb == 3 and k % 4 == 3:
                        n2["slot"] += 1
                        if n2q and n2q[0][2] <= n2["slot"]:
                            g_ = n2q.pop(0)
                            norm2_grp(g_[0], g_[1])
                T.tt("dve", x2acc[:, r, nb * 512:(nb + 1) * 512], ps[:, :], xr, ALU.add)
                T.act(dummy6, x2acc[:, r, nb * 512:(nb + 1) * 512], AF.Square, accum_out=ssq2[:, 4 * r + nb:4 * r + nb + 1])
                if nb == 3:
                    norm2_stats(r)
                    for kb in range(4):
                        n2q.append((r, kb, n2["slot"] + 3))
                it += 1
        for g_ in n2q:
            norm2_grp(g_[0], g_[1])
        T.tt("dve", x2Tm, psb[7][:, 0:32], xTm, ALU.add)
        T.act(sqm, x2Tm, AF.Square)
        for k in range(16):
            T.mm(psb[6][:, 0:2], onesf, sqm[:, 2 * k:2 * k + 2], start=(k == 0), stop=(k == 15))
        T.act(rm, psb[6][:, 0:2], AF.Sqrt, bias=epsc, scale=1.0 / D)
        T.recip(rm, rm)
        x2Tm3 = x2Tm.rearrange("p (k t) -> p k t", k=16)
        T.tt("dve", x2Tm3, x2Tm3, rm.unsqueeze(1).to_broadcast([128, 16, 2]), ALU.mult)
        T.tt("dve", h2T[:, :, 0:2], x2Tm3, bc_mid(gF, 2), ALU.mult)

        sfst = [V(O_C + piece * 5632, [1408]) for piece in range(4)]
        for piece in range(4):
            T.dma("sp", sfst[piece][0:32, :], sf_d[:, piece * 1408:(piece + 1) * 1408])
        for piece in range(4):
            sfstage = sfst[piece]
            pbk = psb[4 + piece % 2]
            for cc in range(11):
                T.tr(pbk[:, cc * 32:(cc + 1) * 32], sfstage[0:32, cc * 128:(cc + 1) * 128], ident[0:32, 0:32])
            T.copy("dve", sfT[:, piece * 11:(piece + 1) * 11, :].rearrange("p a b -> p (a b)"), pbk[:, 0:352])
        NTF = [(128, 640), (640, 1152), (1152, 1282)]
        gstate = {"it": 0, "dit": 0}

        def gateup(grp):
            for hg in range(2):
                gslot = wget()
                uslot = wget(second=True)
                for ci in range(2):
                    f = grp * 4 + hg * 2 + ci
                    fi = hg * 2 + ci
                    ab = actb[grp % 2]
                    for ti, (n0, n1) in enumerate(NTF):
                        it = gstate["it"]
                        gstate["it"] += 1
                        W = n1 - n0
                        s0 = (it % 2) * 2
                        pG, pU = psb[s0], psb[s0 + 1]
                        for k in range(16):
                            T.mm(pG[:, 0:W], gslot[:, k, ci * 128:(ci + 1) * 128], h2T[:, k, n0 - 128:n1 - 128],
                                 start=(k == 0), stop=(k == 15))
                        for k in range(16):
                            T.mm(pU[:, 0:W], uslot[:, k, ci * 128:(ci + 1) * 128], h2T[:, k, n0 - 128:n1 - 128],
                                 start=(k == 0), stop=(k == 15))
                        T.copy("act", g_sb[:, n0:n1], pG[:, 0:W])
                        lo, hi = max(n0, 130), min(n1, 1154)
                        if hi > lo:
                            L = hi - lo
                            tA = tAb[it % 2][:, 0:L]
                            T.act(tA, g_sb[:, lo - 2:hi - 2], AF.Copy, scale=fw(0, f))
                            T.stt("dve", tA, g_sb[:, lo - 1:hi - 1], fw(1, f), tA, ALU.mult, ALU.add)
                            T.stt("dve", tA, g_sb[:, lo:hi], fw(2, f), tA, ALU.mult, ALU.add)
                            tS = tSb[:, 0:L]
                            T.act(tS, tA, AF.Silu, bias=fb(f))
                            T.tt("dve", ab[:, fi, lo - 130:hi - 130], pU[:, lo - n0:hi - n0], tS, ALU.mult)
                        if ti == 2:
                            T.copy("dve", gs_ext[:, :, 2:10], g_sb[:, 1154:1282].rearrange("p (s t) -> p s t", s=16))
                            T.copy("act", gs_ext[:, :, 0:2], sfT[:, f, :].rearrange("p (s r) -> p s r", s=16))
                            tA3 = tAb[(it + 1) % 2][:, 0:128].rearrange("p (s t) -> p s t", s=16)
                            T.act(tA3, gs_ext[:, :, 0:8], AF.Copy, scale=fw(0, f))
                            T.stt("dve", tA3, gs_ext[:, :, 1:9], fw(1, f), tA3, ALU.mult, ALU.add)
                            T.stt("dve", tA3, gs_ext[:, :, 2:10], fw(2, f), tA3, ALU.mult, ALU.add)
                            tS3 = tSb[:, 0:128].rearrange("p (s t) -> p s t", s=16)
                            T.act(tS3, tA3, AF.Silu, bias=fb(f))
                            T.tt("dve", ab[:, fi, 1024:1152].rearrange("p (s t) -> p s t", s=16),
                                 pU[:, 2:130].rearrange("p (s t) -> p s t", s=16), tS3, ALU.mult)
                            T.copy("act", ffn_keep[:, f, 0:2], g_sb[:, 1152:1154])
                            T.copy("act", ffn_keep[:, f, 2:34].rearrange("p (s r) -> p s r", s=16), gs_ext[:, :, 8:10])

        gfin_bc = V(O_B, [2048])
        dummy8 = V(O_B + 8192, [512], BF16)

        def final_tile(r):
            x2 = x2acc[:, r, :]
            ssq = scol()
            T.reduce("dve", ssq, ssq2[:, 4 * r:4 * r + 4], ALU.add)
            rs = rstd_of(ssq, 128, 1.0 / D)
            T.stt("dve", x2[:, 0:1024], x2[:, 0:1024], rs, gfin_bc[:, 0:1024], ALU.mult, ALU.mult)
            T.act(x2[:, 1024:2048], x2[:, 1024:2048], AF.Copy, scale=rs)
            T.tt("pool", x2[:, 1024:2048], x2[:, 1024:2048], gfin_bc[:, 1024:2048], ALU.mult)
            if r < 8:
                T.dma("sp", yp_d[r * 128:(r + 1) * 128, :], x2)
            else:
                T.dma("sp", ys_d, x2)

        def down(grp, last=False):
            dsl = [wget(), wget(second=True)]
            ab = actb[grp % 2]
            if last:
                T.dma("sp", gfin_bc, gfin_d.partition_broadcast(128))
            for r in range(9):
                for nb in range(4):
                    ps = psb[4 + gstate["dit"] % 4]
                    gstate["dit"] += 1
                    for fi in range(4):
                        T.mm(ps[:, :], ab[:, fi, r * 128:(r + 1) * 128], dsl[fi // 2][:, fi % 2, nb * 512:(nb + 1) * 512],
                             start=(fi == 0), stop=(fi == 3))
                    T.tt("dve", x2acc[:, r, nb * 512:(nb + 1) * 512], ps[:, :], x2acc[:, r, nb * 512:(nb + 1) * 512], ALU.add)
                    if last:
                        T.act(dummy8, x2acc[:, r, nb * 512:(nb + 1) * 512], AF.Square,
                              accum_out=ssq2[:, 4 * r + nb:4 * r + nb + 1])
                if last and r >= 1:
                    final_tile(r - 1)
            if last:
                final_tile(8)

        for grp in range(11):
            gateup(grp)
            if grp >= 1:
                down(grp - 1)
        for rnd in range(3):
            chunks = list(range(rnd * 16, min(NFF, rnd * 16 + 16)))
            for cc, f in enumerate(chunks):
                T.tr(psb[cc // 4][0:34, (cc % 4) * 128:(cc % 4 + 1) * 128], ffn_keep[:, f, :], ident)
            nbk = (len(chunks) + 3) // 4
            for b in range(nbk):
                T.copy("act" if b % 2 == 0 else "dve", ffstage[0:34, b * 512:(b + 1) * 512], psb[b][0:34, :])
            ncols = len(chunks) * 128
            T.dma("sp", ffp_d[:, rnd * 2048:rnd * 2048 + ncols], ffstage[0:2, 0:ncols])
            T.dma("sp", ffs_d[:, rnd * 2048:rnd * 2048 + ncols], ffstage[2:34, 0:ncols])

        down(10, last=True)
        assert wstate["cons"] == len(wspecs), (wstate, len(wspecs))
        T.emit()
    return nc


def _const_tables(sinks):
    slopes = np.exp2(-8.0 * np.arange(1, 17, dtype=np.float32) / 16.0).astype(np.float32)
    q = np.arange(128)[:, None]
    k = np.arange(256)[None, :]
    dist = (q - k + 128).astype(np.float32)
    valid = (dist >= 0) & (dist <= 128)
    btab = np.empty((128, 16, 257), np.float32)
    for e, h in enumerate(PERM_HEADS):
        btab[:, e, 0:256] = np.where(valid, -slopes[h] * dist, NEG)
        btab[:, e, 256] = sinks[h]
    r = np.arange(128)
    t = (r % 8)[:, None]
    kj = np.arange(136)[None, :]
    dist_s = (t + 128 - kj).astype(np.float32)
    valid_s = (dist_s >= 0) & (dist_s <= 128)
    btabs = np.empty((128, 137), np.float32)
    btabs[:, 0:136] = np.where(valid_s, -slopes[r // 8][:, None] * dist_s, NEG)
    btabs[:, 136] = sinks[r // 8]
    i = np.arange(2)[:, None]
    kk = np.arange(130)[None, :]
    dist_m = (128 + i - kk).astype(np.float32)
    valid_m = (dist_m >= 0) & (dist_m <= 128)
    bmini = np.empty((2, 16, 131), np.float32)
    for e, h in enumerate(PERM_HEADS):
        bmini[:, e, 0:130] = np.where(valid_m, -slopes[h] * dist_m, NEG)
        bmini[:, e, 130] = sinks[h]
    return btab.reshape(128, 16 * 257), btabs, bmini.reshape(2, 16 * 131)


_NC_CACHE = {}


def kernel(x_prompt, x_sample, cache_k_window, cache_v_window, state_conv, state_ffn_conv,
           g_attn_norm, w_in, attn_sinks, conv_w, g_out_attn, g_out_conv, w_out,
           g_ffn_norm, w_gate, w_up, ffn_conv_w, ffn_conv_b, w_down, g_final):
    f32 = np.float32
    x_prompt = np.asarray(x_prompt, f32)
    x_sample = np.asarray(x_sample, f32)
    featperm = np.concatenate([np.arange(h * 64, (h + 1) * 64) for h in PERM_HEADS])
    w_in0 = np.asarray(w_in, f32)[0]
    cols = []
    for c in range(8):
        for base in (1536, 2560, 3584):
            cols.append(np.arange(base + c * 128, base + (c + 1) * 128))
    cols.append(np.arange(1024, 1536))
    cols.append(featperm)
    win = np.ascontiguousarray(w_in0[:, np.concatenate(cols)])
    w_out0 = np.asarray(w_out, f32)[0]
    wout = np.ascontiguousarray(np.concatenate([w_out0[featperm], w_out0[1024:]], axis=0))
    wg = np.ascontiguousarray(np.asarray(w_gate, f32)[0])
    wu = np.ascontiguousarray(np.asarray(w_up, f32)[0])
    wd = np.ascontiguousarray(np.asarray(w_down, f32)[0])

    def fm(v, n):
        return np.asarray(v, f32).reshape(n, 128).T

    vecs = np.concatenate([
        fm(np.asarray(g_attn_norm)[0], 16), fm(np.asarray(g_ffn_norm)[0], 16),
        fm(np.asarray(g_out_attn, f32)[0][featperm], 8), fm(np.asarray(g_out_conv)[0], 8),
        np.concatenate([fm(np.asarray(conv_w)[0, j], 8) for j in range(3)], axis=1),
        np.concatenate([fm(np.asarray(ffn_conv_w)[0, j], NFF) for j in range(3)], axis=1),
        fm(np.asarray(ffn_conv_b)[0], NFF)], axis=1).astype(f32)
    assert vecs.shape == (128, V_N)
    vecs = np.ascontiguousarray(vecs)
    sinks = np.asarray(attn_sinks, f32)[0]
    sinkbc = np.ascontiguousarray(np.broadcast_to(sinks[PERM_HEADS][None, :], (128, 16)))
    sinkrows = np.ascontiguousarray(np.repeat(sinks, 8)[:, None])
    btab, btabs, bmini = _const_tables(sinks)
    idn = np.eye(128, dtype=f32)
    gfin = np.ascontiguousarray(np.asarray(g_final, f32))
    ck_all = np.asarray(cache_k_window, f32)[0].reshape(128, 128, 256)
    cv_all = np.asarray(cache_v_window, f32)[0].reshape(128, 128, 256)
    sc_all = np.asarray(state_conv, f32)[0]
    sf_all = np.asarray(state_ffn_conv, f32)[0]

    in_maps = []
    for c in range(8):
        b, half = c // 2, c % 2
        xin = np.zeros((NT, D), f32)
        if half == 1:
            xin[0:130] = x_prompt[b, 1024 - 130:1024]
        xin[130:1154] = x_prompt[b, half * 1024:(half + 1) * 1024]
        xin[1154:1282] = x_sample[16 * c:16 * (c + 1)].reshape(128, D)
        hm = np.full((128, 130), NEG if half == 0 else 0.0, f32)
        in_maps.append({
            "xin": xin,
            "ck": np.ascontiguousarray(ck_all[16 * c:16 * (c + 1)]),
            "cv": np.ascontiguousarray(cv_all[16 * c:16 * (c + 1)]),
            "sc": np.ascontiguousarray(sc_all[16 * c:16 * (c + 1)].reshape(32, 1024)),
            "sf": np.ascontiguousarray(sf_all[16 * c:16 * (c + 1)].reshape(32, DFF)),
            "win": win, "wout": wout, "wg": wg, "wu": wu, "wd": wd,
            "vecs": vecs, "sinkbc": sinkbc, "sinkrows": sinkrows,
            "btab": btab, "btabs": btabs, "bmini": bmini, "hmask": hm, "idn": idn, "gfin": gfin,
        })

    if "nc" not in _NC_CACHE:
        _NC_CACHE["nc"] = build_nc()
    nc = _NC_CACHE["nc"]
    res = run_bass_kernel_spmd(nc, in_maps, core_ids=list(range(8)))
    R = res.results

    y_prompt = np.empty((4, 2048, D), f32)
    y_sample = np.empty((128, 8, D), f32)
    kwp = np.empty((1, 4, 128, 4, 64), f32)
    vwp = np.empty((1, 4, 128, 4, 64), f32)
    cvp = np.empty((1, 4, 2, 1024), f32)
    ffp = np.empty((1, 4, 2, DFF), f32)
    kws = np.empty((1, 128, 128, 4, 64), f32)
    vws = np.empty((1, 128, 128, 4, 64), f32)
    cvs = np.empty((1, 128, 2, 1024), f32)
    ffs = np.empty((1, 128, 2, DFF), f32)
    for c in range(8):
        b, half = c // 2, c % 2
        r = R[c]
        y_prompt[b, half * 1024:(half + 1) * 1024] = r["yp"]
        y_sample[16 * c:16 * (c + 1)] = r["ys"].reshape(16, 8, D)
        if half == 1:
            kwp[0, b] = r["kwp"].reshape(128, 4, 64)
            vwp[0, b] = r["vwp"].reshape(128, 4, 64)
            cvp[0, b] = r["cvp"]
            ffp[0, b] = r["ffp"]
        kws[0, 16 * c:16 * (c + 1)] = r["kws"].reshape(16, 128, 4, 64)
        vws[0, 16 * c:16 * (c + 1)] = r["vws"].reshape(16, 128, 4, 64)
        cvs[0, 16 * c:16 * (c + 1)] = r["cvs"].reshape(16, 2, 1024)
        ffs[0, 16 * c:16 * (c + 1)] = r["ffs"].reshape(16, 2, DFF)
    return (y_prompt, y_sample, kwp, vwp, cvp, ffp, kws, vws, cvs, ffs)
```

```python
import numpy as np
from contextlib import ExitStack
import concourse.bass as bass
import concourse.mybir as mybir
from concourse.bass_utils import run_bass_kernel_spmd

F32 = mybir.dt.float32
BF16 = mybir.dt.bfloat16
ALU = mybir.AluOpType
AF = mybir.ActivationFunctionType
AX = mybir.AxisListType
_ESZ = {F32: 4, BF16: 2}

D = 2048
DFF = 5632
NFF = 44
NT = 1282
EPS = 1e-6
NEG = -1e30

PERM_HEADS = [4 * (2 * (qc // 4) + half) + (qc % 4) for qc in range(8) for half in range(2)]


class _Op:
    __slots__ = ("eng", "fn", "deps", "dmadeps", "sig", "semval", "dma", "dsem", "dval", "prev_same_sem", "vc")


class Tracker:
    CE = ("pe", "act", "dve", "pool")

    def __init__(self, nc, es, same_engine_sync=True, nq=12):
        self.nc = nc
        self.same = same_engine_sync
        self.fold = True
        self.last_on = {}
        self.eobj = {"pe": nc.tensor, "act": nc.scalar, "dve": nc.vector, "pool": nc.gpsimd, "sp": nc.sync}
        self.ops = []
        self.recs = {}
        self.dram = set()
        self.csem = {e: es.enter_context(nc.semaphore("s_" + e)) for e in self.CE}
        self.nq = nq
        self.qsem = {q: [es.enter_context(nc.semaphore(f"q_{q}{i}")) for i in range(nq)] for q in ("sp", "pool")}
        self.qcnt = {q: 0 for q in self.qsem}
        self.qlast = {q: [None] * nq for q in self.qsem}

    def region(self, ap):
        name = ap.tensor.name
        if name in self.dram:
            return None
        if name.startswith("psb"):
            return (name, 0, 128, 0, 2048)
        esz = _ESZ[ap.dtype]
        pat = ap.ap
        off = int(ap.offset)
        ps, pn = pat[0]
        if ps == 0:
            ps = 1 << 40
        p0 = off // ps
        f0 = off % ps
        ext = 0
        for st, cnt in pat[1:]:
            ext += (cnt - 1) * abs(st)
        return (name, p0, p0 + pn, f0 * esz, (f0 + ext + 1) * esz)

    def add(self, eng, fn, reads, writes, dma=None):
        op = _Op()
        op.eng = eng
        op.fn = fn
        op.dma = dma
        op.sig = False
        op.semval = None
        idx = len(self.ops)
        deps = {}
        dmadeps = set()
        ops = self.ops

        def dep_on(j):
            p = ops[j]
            if p.dma is not None:
                dmadeps.add(j)
            elif deps.get(p.eng, -1) < j:
                deps[p.eng] = j

        rregs = [r for r in (self.region(a) for a in reads if a is not None) if r is not None]
        wregs = [r for r in (self.region(a) for a in writes if a is not None) if r is not None]
        for (name, p0, p1, b0, b1) in rregs:
            psum = name.startswith("psb")
            for rec in self.recs.get(name, ()):
                if (rec[5] or (psum and rec[6] != eng)) and rec[0] < p1 and p0 < rec[1] and rec[2] < b1 and b0 < rec[3]:
                    dep_on(rec[4])
        for (name, p0, p1, b0, b1) in wregs:
            lst = self.recs.get(name, [])
            keep = []
            for rec in lst:
                if rec[0] < p1 and p0 < rec[1] and rec[2] < b1 and b0 < rec[3]:
                    dep_on(rec[4])
                    if rec[0] >= p0 and rec[1] <= p1 and rec[2] >= b0 and rec[3] <= b1:
                        continue
                keep.append(rec)
            keep.append([p0, p1, b0, b1, idx, True, eng])
            self.recs[name] = keep
        for (name, p0, p1, b0, b1) in rregs:
            lst = self.recs.setdefault(name, [])
            found = False
            if dma is None:
                for rec in lst:
                    if (not rec[5]) and rec[6] == eng and rec[0] == p0 and rec[1] == p1 and rec[2] == b0 and rec[3] == b1:
                        rec[4] = idx
                        found = True
                        break
            if not found:
                lst.append([p0, p1, b0, b1, idx, False, eng if dma is None else "dma"])
        fdeps = {}
        for e, j in deps.items():
            if e == eng and dma is None and (e == "pe" or not self.same):
                continue
            fdeps[e] = j
        prev = self.last_on.get(eng)
        vc = dict(ops[prev].vc) if prev is not None else {}
        kept = {}
        for e, j in sorted(fdeps.items(), key=lambda kv: -kv[1]):
            if vc.get(e, -1) >= j:
                continue
            kept[e] = j
            ops[j].sig = True
            for ee, jj in ops[j].vc.items():
                if vc.get(ee, -1) < jj:
                    vc[ee] = jj
            vc[e] = j
        for j in dmadeps:
            for ee, jj in ops[j].vc.items():
                if vc.get(ee, -1) < jj:
                    vc[ee] = jj
        op.vc = vc
        self.last_on[eng] = idx
        op.deps = kept
        op.dmadeps = dmadeps
        if dma is not None:
            q = dma
            c = self.qcnt[q]
            self.qcnt[q] = c + 1
            slot = c % self.nq
            op.dsem = self.qsem[q][slot]
            op.dval = 16 * (c // self.nq + 1)
            op.prev_same_sem = self.qlast[q][slot]
            self.qlast[q][slot] = idx
        ops.append(op)
        return idx

    def emit(self):
        cnt = {e: 0 for e in self.CE}
        for op in self.ops:
            if op.dma is None and op.sig:
                cnt[op.eng] += 1
                op.semval = cnt[op.eng]
        water = {}

        def wait(eng, sem, val):
            key = (eng, id(sem))
            if water.get(key, 0) >= val:
                return
            water[key] = val
            self.eobj[eng].wait_ge(sem, val)

        for op in self.ops:
            eng = op.eng
            need = []
            for e, j in op.deps.items():
                need.append((self.csem[e], self.ops[j].semval))
            for j in op.dmadeps:
                p = self.ops[j]
                need.append((p.dsem, p.dval))
            if op.dma is not None and op.prev_same_sem is not None:
                p = self.ops[op.prev_same_sem]
                need.append((p.dsem, p.dval))
            need = [(sm, v) for (sm, v) in need if water.get((eng, id(sm)), 0) < v]
            own = need.pop() if (need and op.dma is None and self.fold) else None
            for sm, v in need:
                wait(eng, sm, v)
            inst = op.fn()
            if own is not None:
                water[(eng, id(own[0]))] = own[1]
                inst._wait_ge(own[0], own[1])
            if op.dma is not None:
                inst.then_inc(op.dsem, 16)
            elif op.sig:
                inst.then_inc(self.csem[eng], 1)
        for q in self.qsem:
            for slot in range(self.nq):
                j = self.qlast[q][slot]
                if j is not None:
                    p = self.ops[j]
                    wait("sp", p.dsem, p.dval)
        return cnt

    def dma(self, q, out, in_, after=()):
        eo = self.eobj[q]
        return self.add(q, lambda: eo.dma_start(out=out, in_=in_), [in_, *after], [out], dma=q)

    def mm(self, out, lhsT, rhs, start=True, stop=True):
        nc = self.nc
        return self.add("pe", lambda: nc.tensor.matmul(out, lhsT, rhs, start=start, stop=stop), [lhsT, rhs], [out])

    def tr(self, out, in_, ident):
        nc = self.nc
        return self.add("pe", lambda: nc.tensor.transpose(out, in_, ident), [in_, ident], [out])

    def act(self, out, in_, func, bias=None, scale=1.0, accum_out=None):
        nc = self.nc
        kw = {}
        rd = [in_]
        if bias is not None:
            kw["bias"] = bias
            if not isinstance(bias, (int, float)):
                rd.append(bias)
        if not isinstance(scale, (int, float)):
            rd.append(scale)
        kw["scale"] = scale
        wr = [out]
        if accum_out is not None:
            kw["accum_out"] = accum_out
            wr.append(accum_out)
        return self.add("act", lambda: nc.scalar.activation(out=out, in_=in_, func=func, **kw), rd, wr)

    def tt(self, eng, out, in0, in1, op):
        eo = self.eobj[eng]
        return self.add(eng, lambda: eo.tensor_tensor(out=out, in0=in0, in1=in1, op=op), [in0, in1], [out])

    def ts(self, eng, out, in0, s1, s2, op0, op1=None):
        eo = self.eobj[eng]
        rd = [in0]
        if not isinstance(s1, (int, float)):
            rd.append(s1)
        if s2 is not None and not isinstance(s2, (int, float)):
            rd.append(s2)
        kw = {}
        if op1 is not None:
            kw["op1"] = op1
        return self.add(eng, lambda: eo.tensor_scalar(out=out, in0=in0, scalar1=s1, scalar2=s2, op0=op0, **kw), rd, [out])

    def stt(self, eng, out, in0, scalar, in1, op0, op1):
        eo = self.eobj[eng]
        rd = [in0, in1]
        if not isinstance(scalar, (int, float)):
            rd.append(scalar)
        return self.add(eng, lambda: eo.scalar_tensor_tensor(out=out, in0=in0, scalar=scalar, in1=in1, op0=op0, op1=op1), rd, [out])

    def copy(self, eng, out, in_):
        if eng == "act":
            nc = self.nc
            return self.add("act", lambda: nc.scalar.copy(out=out, in_=in_), [in_], [out])
        eo = self.eobj[eng]
        return self.add(eng, lambda: eo.tensor_copy(out=out, in_=in_), [in_], [out])

    def reduce(self, eng, out, in_, op):
        eo = self.eobj[eng]
        return self.add(eng, lambda: eo.tensor_reduce(out=out, in_=in_, axis=AX.X, op=op), [in_], [out])

    def recip(self, out, in_):
        nc = self.nc
        return self.add("dve", lambda: nc.vector.reciprocal(out=out, in_=in_), [in_], [out])

    def memset(self, eng, ap, val):
        eo = self.eobj[eng]
        return self.add(eng, lambda: eo.memset(ap, val), [], [ap])


V_GA, V_GF, V_GOA, V_GOC, V_CW, V_FW, V_FB, V_N = 0, 16, 32, 40, 48, 72, 204, 248

ARENA_BYTES = 212000
O_CONST = 0
O_JUNK = 4096
O_WR = 8192
O_A = 57344
O_B = 131072
O_C = 175008
assert O_C + 36992 <= ARENA_BYTES


def build_nc(same_engine_sync=True):
    nc = bass.Bass("TRN2", target_bir_lowering=False)
    with ExitStack() as es:
        T = Tracker(nc, es, same_engine_sync=same_engine_sync)

        def DI(name, shape):
            t = nc.dram_tensor(name, list(shape), F32, kind="ExternalInput")
            T.dram.add(t.name)
            return t.ap()

        def DO(name, shape):
            t = nc.dram_tensor(name, list(shape), F32, kind="ExternalOutput")
            T.dram.add(t.name)
            return t.ap()

        xin = DI("xin", (NT, D))
        ck_d = DI("ck", (16, 128, 256))
        cv_d = DI("cv", (16, 128, 256))
        sc_d = DI("sc", (32, 1024))
        sf_d = DI("sf", (32, DFF))
        win_d = DI("win", (D, 4608))
        wout_d = DI("wout", (D, D))
        wg_d = DI("wg", (D, DFF))
        wu_d = DI("wu", (D, DFF))
        wd_d = DI("wd", (DFF, D))
        vecs_d = DI("vecs", (128, V_N))
        sinkbc_d = DI("sinkbc", (128, 16))
        sinkrows_d = DI("sinkrows", (128, 1))
        btab_d = DI("btab", (128, 16 * 257))
        btabs_d = DI("btabs", (128, 137))
        bmini_d = DI("bmini", (2, 16 * 131))
        hmask_d = DI("hmask", (128, 130))
        idn_d = DI("idn", (128, 128))
        gfin_d = DI("gfin", (D,))

        yp_d = DO("yp", (1024, D))
        ys_d = DO("ys", (128, D))
        kwp_d = DO("kwp", (128, 256))
        vwp_d = DO("vwp", (128, 256))
        cvp_d = DO("cvp", (2, 1024))
        ffp_d = DO("ffp", (2, DFF))
        kws_d = DO("kws", (16, 128, 256))
        vws_d = DO("vws", (16, 128, 256))
        cvs_d = DO("cvs", (32, 1024))
        ffs_d = DO("ffs", (32, DFF))

        arena = es.enter_context(nc.sbuf_tensor("arena", [128, ARENA_BYTES // 4], F32))
        psb = [es.enter_context(nc.psum_tensor(f"psb{i}", [128, 512], F32)) for i in range(8)]

        def V(off, shape, dt=F32):
            n = 1
            for s in shape:
                n *= s
            nb = n * _ESZ[dt]
            assert off % 4 == 0 and nb % 4 == 0 and off + nb <= ARENA_BYTES, (off, shape)
            a = arena[:, off // 4:(off + nb) // 4]
            if dt != F32:
                a = a.bitcast(dt)
            if len(shape) == 2:
                a = a.rearrange("p (a b) -> p a b", a=shape[0])
            elif len(shape) == 3:
                a = a.rearrange("p (a b c) -> p a b c", a=shape[0], b=shape[1])
            return a

        def PSB(i):
            return psb[i][:, :].bitcast(BF16)

        o = O_CONST
        ident = V(o, [128]); o += 512
        identb = V(o, [128], BF16); o += 256
        onesf = V(o, [128]); o += 512
        vecs = V(o, [V_N]); o += V_N * 4
        sinkbc = V(o, [16]); o += 64
        sinkrows = V(o, [1]); o += 4
        epsc = V(o, [1]); o += 4
        hmask = V(o, [130]); o += 520
        xTm = V(o, [32]); o += 128
        rinv = V(o, [16]); o += 64
        NSC = 192
        stats = V(o, [NSC]); o += NSC * 4
        ssq2 = V(o, [36]); o += 144
        assert o <= O_JUNK, o
        junk = V(O_JUNK, [2048], BF16)
        scnt = [0]

        def scol():
            i = scnt[0] % NSC
            scnt[0] += 1
            return stats[:, i:i + 1]

        gA = vecs[:, V_GA:V_GA + 16]
        gF = vecs[:, V_GF:V_GF + 16]
        gOA = vecs[:, V_GOA:V_GOA + 8]
        gOC = vecs[:, V_GOC:V_GOC + 8]

        def cw(j, c):
            return vecs[:, V_CW + j * 8 + c:V_CW + j * 8 + c + 1]

        def fw(j, f):
            return vecs[:, V_FW + j * NFF + f:V_FW + j * NFF + f + 1]

        def fb(f):
            return vecs[:, V_FB + f:V_FB + f + 1]

        T.dma("sp", ident, idn_d)
        T.dma("pool", identb, idn_d)
        T.dma("sp", vecs, vecs_d)
        T.dma("sp", sinkbc, sinkbc_d)
        T.dma("sp", sinkrows, sinkrows_d)
        T.dma("sp", hmask, hmask_d)
        T.memset("dve", onesf, 1.0)
        T.memset("dve", epsc, EPS)

        wspecs = []

        def wv16(i, cols):
            return V(O_WR + (i % 3) * 16384, [16, cols], BF16)

        def wv8(i, shape):
            return V(O_WR + (i % 6) * 8192, shape, BF16)

        win_v = win_d.rearrange("(k p) c -> p k c", p=128)
        wout_v = wout_d.rearrange("(k p) c -> p k c", p=128)
        wg_v = wg_d.rearrange("(k p) c -> p k c", p=128)
        wu_v = wu_d.rearrange("(k p) c -> p k c", p=128)
        wd_v = wd_d.rearrange("(c p) n -> p c n", p=128)
        li = 0
        for c in range(8):
            wspecs.append((win_v[:, :, c * 384:(c + 1) * 384], wv16(li, 384))); li += 1
        wspecs.append((win_v[:, :, 3072:3584], wv16(li, 512))); li += 1
        for h in range(2):
            wspecs.append((win_v[:, :, 3584 + h * 512:3584 + (h + 1) * 512], wv16(li, 512))); li += 1
        for nb in range(4):
            wspecs.append((wout_v[:, :, nb * 512:(nb + 1) * 512], wv16(li, 512))); li += 1
        N16 = li
        fi_ = 0
        def _dspecs(grp):
            nonlocal fi_
            for hg in range(2):
                f0 = grp * 4 + hg * 2
                wspecs.append((wd_v[:, f0:f0 + 2, :], wv8(fi_, [2, 2048]))); fi_ += 1

        for grp in range(11):
            for hg in range(2):
                c0 = (grp * 4 + hg * 2) * 128
                wspecs.append((wg_v[:, :, c0:c0 + 256], wv8(fi_, [16, 256]))); fi_ += 1
                wspecs.append((wu_v[:, :, c0:c0 + 256], wv8(fi_, [16, 256]))); fi_ += 1
            if grp >= 1:
                _dspecs(grp - 1)
        _dspecs(10)
        wstate = {"issued": 0, "cons": 0}

        def wget(second=False):
            i = wstate["cons"]
            wstate["cons"] += 1
            floor = i - 1 if second else i
            def sz(n):
                return 16384 if n < N16 else 8192
            while wstate["issued"] < len(wspecs):
                n = wstate["issued"]
                if n > i and sum(sz(m) for m in range(floor, n + 1)) > 49152:
                    break
                src, dst = wspecs[n]
                T.dma("pool", dst, src, after=([xstage[1][:, :]] if n in (1, 2) else ()))
                wstate["issued"] += 1
            return wspecs[i][1]

        hT = V(O_B, [16, NT], BF16)
        xstage = [V(O_C + i * 8192, [2048]) for i in range(4)]
        mixedT = V(O_C, [16, 1156], BF16)
        sconv = V(O_A, [8, 1156])
        rstd_bc = V(O_A + 36992, [1156])
        sqb = [V(O_A + 41616 + i * 2048, [512]) for i in range(2)]
        cu_keep = V(O_A + 45712, [8, 34])
        convout = V(O_A + 46800, [1024])
        eA = O_A + 54648
        cu = V(eA, [NT])
        tmpU = [V(eA + 5128 + i * 2048, [512]) for i in range(2)]
        tmpA = [V(eA + 9224 + i * 2048, [512]) for i in range(2)]
        scT = V(eA + 13320, [8, 32])
        scstage = V(eA + 14344, [1024])
        cus_ext = V(eA + 18440, [16, 10])
        assert eA + 18440 + 640 <= O_A + 73728
        Vc = V(O_A, [16, 256], BF16)
        KTs = V(O_A + 8192, [2, 16, 136], BF16)
        Vnew = V(O_A + 16896, [16, 256], BF16)
        QT = V(O_A + 25088, [8, NT], BF16)
        QSpad = V(O_A + 45600, [2, 16, 128], BF16)
        assert O_A + 45600 + 8192 <= eA
        KT = V(eA, [2, NT], BF16)
        vtm = V(eA + 5128, [11, 256], BF16)
        kc_tm = V(eA + 10760, [16, 256], BF16)
        assert eA + 10760 + 8192 <= O_A + 73728
        kvstage = [V(O_JUNK + i * 2048, [512]) for i in range(2)]
        btab = V(O_B, [16, 257])
        btabs = V(O_B + 16448, [137])
        bmini = V(O_B + 16448 + 548, [16, 131])
        ob = O_B + 16448 + 548 + 8384
        S_sb = [V(ob + i * 1032, [258]) for i in range(2)]; ob += 2064
        P_sb = [V(ob + i * 520, [260], BF16) for i in range(2)]; ob += 1040
        Pf_sb = [V(ob + i * 552, [138]) for i in range(2)]; ob += 1104
        PT_sb = [V(ob + i * 512, [256], BF16) for i in range(2)]; ob += 1024
        o_attn_sb = ob
        attn_sb = V(ob, [1024]); ob += 4096
        attn_sc = V(ob, [1024]); ob += 4096
        rb_s = V(ob, [128]); ob += 512
        rs_all = [V(ob + i * 64, [16]) for i in range(2)]; ob += 128
        ng_all = [V(ob + i * 64, [16]) for i in range(2)]; ob += 128
        assert ob <= O_B + 41024, ob
        attnT_s = V(eA + 10760, [8, 128])
        sq_s = V(eA + 10760 + 4096, [1024])
        x2acc = V(O_A, [9, 2048])
        h2T = V(O_B, [16, 1154], BF16)
        xres = [V(O_JUNK + i * 2048, [512]) for i in range(2)]
        xs2b = [V(O_C + i * 8192, [2048]) for i in range(2)]
        x2Tm = V(O_B + 36928, [32])
        sqm = V(O_B + 36928 + 128, [32])
        rm = V(O_B + 36928 + 256, [2])
        actb = [V(O_C + i * 9216, [4, 1152], BF16) for i in range(2)]
        g_sb = V(O_C + 18432, [NT])
        tAb = [V(O_C + 23560 + i * 2048, [512]) for i in range(2)]
        tSb = V(O_C + 27656, [512])
        sfT = V(O_C + 29704, [NFF, 32])
        gs_ext = V(O_C + 35336, [16, 10])
        assert O_C + 35336 + 640 <= ARENA_BYTES
        ffn_keep = V(O_B + 36928 + 512, [NFF, 34])
        assert O_B + 36928 + 512 + NFF * 34 * 4 <= O_C
        sfstage = V(O_C, [1408])
        ffstage = V(O_B + 16384, [2048])

        def bc_mid(ap, n):
            return ap.unsqueeze(2).to_broadcast([ap.shape[0], ap.shape[1], n])

        def rstd_of(ssq_ap, rows, scale):
            a = scol()
            b = scol()
            T.act(a[0:rows], ssq_ap, AF.Sqrt, bias=epsc[0:rows], scale=scale)
            T.recip(b[0:rows], a[0:rows])
            return b

        tilesA = [(0, 2)] + [(2 + 128 * i, 130 + 128 * i) for i in range(9)] + [(1154, 1282)]
        T.dma("sp", xstage[3][0:2, :], xin[128:130, :])
        for k in range(16):
            T.tr(psb[7][:, 2 * k:2 * k + 2], xstage[3][0:2, k * 128:(k + 1) * 128], ident[0:2, 0:2])
        T.copy("dve", xTm, psb[7][:, 0:32])
        def norm1_tile(i):
            c0, c1 = tilesA[i]
            rows = c1 - c0
            xs_ = xstage[i % 4]
            T.dma("sp", xs_[0:rows, :], xin[c0:c1, :])
            ssq = scol()
            T.act(junk[0:rows, :], xs_[0:rows, :], AF.Square, accum_out=ssq[0:rows])
            rs = rstd_of(ssq[0:rows], rows, 1.0 / D)
            T.act(xs_[0:rows, :], xs_[0:rows, :], AF.Copy, scale=rs[0:rows])
            for kb in range(4):
                bank = (i % 2) * 4 + kb
                for j in range(4):
                    k = kb * 4 + j
                    T.tr(psb[bank][:, j * 128:j * 128 + rows], xs_[0:rows, k * 128:(k + 1) * 128], ident[0:rows, 0:rows])
                T.tt("dve", hT[:, kb * 4:kb * 4 + 4, c0:c1],
                     psb[bank][:, :].rearrange("p (a b) -> p a b", a=4)[:, :, 0:rows],
                     bc_mid(gA[:, kb * 4:kb * 4 + 4], rows), ALU.mult)

        for i in range(6):
            norm1_tile(i)
        T.dma("sp", kws_d[:, 0:120, :], ck_d[:, 8:128, :])
        T.dma("sp", vws_d[:, 0:120, :], cv_d[:, 8:128, :])
        T.dma("sp", scstage[0:32, :], sc_d)
        for c in range(8):
            T.tr(psb[6][:, c * 32:(c + 1) * 32], scstage[0:32, c * 128:(c + 1) * 128], ident[0:32, 0:32])
        T.copy("dve", scT.rearrange("p a b -> p (a b)"), psb[6][:, 0:256])
        NTQ = [(126, 638), (638, 1150), (1150, 1282)]
        it = 0
        for c in range(8):
            slot = wget()
            for ti, (n0, n1) in enumerate(NTQ):
                if c == 0 and ti == 1:
                    for i in range(6, 10):
                        norm1_tile(i)
                if c == 0 and ti == 2:
                    norm1_tile(10)
                W = n1 - n0
                pB, pC, pU = psb[(3 * it) % 8], psb[(3 * it + 1) % 8], psb[(3 * it + 2) % 8]
                for wi, pX in enumerate((pB, pC, pU)):
                    for k in range(16):
                        T.mm(pX[:, 0:W], slot[:, k, wi * 128:(wi + 1) * 128], hT[:, k, n0:n1], start=(k == 0), stop=(k == 15))
                tU = tmpU[it % 2][:, 0:W]
                T.copy("act", tU, pU[:, 0:W])
                T.tt("dve", cu[:, n0:n1], pC[:, 0:W], tU, ALU.mult)
                lo, hi = max(n0, 128), min(n1, 1154)
                if hi > lo:
                    L = hi - lo
                    tA = tmpA[it % 2][:, 0:L]
                    T.act(tA, cu[:, lo - 2:hi - 2], AF.Copy, scale=cw(0, c))
                    T.stt("dve", tA, cu[:, lo - 1:hi - 1], cw(1, c), tA, ALU.mult, ALU.add)
                    T.stt("dve", tA, cu[:, lo:hi], cw(2, c), tA, ALU.mult, ALU.add)
                    T.tt("dve", sconv[:, c, lo - 126:hi - 126], pB[:, lo - n0:hi - n0], tA, ALU.mult)
                if ti == 2:
                    T.copy("dve", cus_ext[:, :, 2:10], cu[:, 1154:1282].rearrange("p (s t) -> p s t", s=16))
                    T.copy("act", cus_ext[:, :, 0:2], scT[:, c, :].rearrange("p (s r) -> p s r", s=16))
                    tA3 = tmpA[(it + 1) % 2][:, 0:128].rearrange("p (s t) -> p s t", s=16)
                    T.act(tA3, cus_ext[:, :, 0:8], AF.Copy, scale=cw(0, c))
                    T.stt("dve", tA3, cus_ext[:, :, 1:9], cw(1, c), tA3, ALU.mult, ALU.add)
                    T.stt("dve", tA3, cus_ext[:, :, 2:10], cw(2, c), tA3, ALU.mult, ALU.add)
                    T.tt("dve", sconv[:, c, 1028:1156].rearrange("p (s t) -> p s t", s=16),
                         pB[:, 4:132].rearrange("p (s t) -> p s t", s=16), tA3, ALU.mult)
                    T.copy("act", cu_keep[:, c, 0:2], cu[:, 1152:1154])
                    T.copy("act", cu_keep[:, c, 2:34].rearrange("p (s r) -> p s r", s=16), cus_ext[:, :, 8:10])
                it += 1
        MT = [(2, 514), (514, 1026), (1026, 1156)]
        it = 0
        for c in range(8):
            for ti, (m0, m1) in enumerate(MT):
                sq = sqb[it % 2][:, 0:m1 - m0]
                T.act(sq, sconv[:, c, m0:m1], AF.Square)
                T.mm(psb[ti][:, 0:m1 - m0], onesf, sq, start=(c == 0), stop=(c == 7))
                it += 1
        for ti, (m0, m1) in enumerate(MT):
            T.act(rstd_bc[:, m0:m1], psb[ti][:, 0:m1 - m0], AF.Sqrt, bias=epsc, scale=1.0 / 1024)
            T.recip(rstd_bc[:, m0:m1], rstd_bc[:, m0:m1])
        for c in range(8):
            T.stt("dve", mixedT[:, 8 + c, 2:1156], sconv[:, c, 2:1156], gOC[:, c:c + 1],
                  rstd_bc[:, 2:1156], ALU.mult, ALU.mult)
        for c in range(8):
            T.tr(psb[6 + c // 4][0:34, (c % 4) * 128:(c % 4 + 1) * 128], cu_keep[:, c, :], ident)
        T.copy("act", convout[0:34, 0:512], psb[6][0:34, :])
        T.copy("act", convout[0:34, 512:1024], psb[7][0:34, :])
        T.dma("sp", cvp_d, convout[0:2, :])
        T.dma("sp", cvs_d, convout[2:34, :])

        slot = wget()
        T.dma("pool", kc_tm, ck_d.rearrange("s k d -> k s d"))
        NTK = [(0, 512), (512, 1024), (1024, 1282)]
        it = 0
        for pc in range(2):
            for ti, (n0, n1) in enumerate(NTK):
                W = n1 - n0
                ps = psb[it % 4]
                for k in range(16):
                    T.mm(ps[:, 0:W], slot[:, k, pc * 128:(pc + 1) * 128], hT[:, k, n0:n1], start=(k == 0), stop=(k == 15))
                T.copy("act" if it % 2 == 0 else "dve", KT[:, pc, n0:n1], ps[:, 0:W])
                it += 1
        for i, (c0, c1) in enumerate(tilesA):
            rows = c1 - c0
            full = i in (9, 10)
            ps = psb[4 + i % 2]
            r0 = 0 if full else 256
            N = 512 - r0
            for k in range(16):
                T.mm(ps[0:rows, 0:N], hT[:, k, c0:c1], slot[:, k, r0:512], start=(k == 0), stop=(k == 15))
            if full:
                stg = kvstage[i % 2]
                T.copy("dve", stg, ps[:, 0:512])
                T.copy("act", vtm[:, i, :], ps[:, 256:512])
                if i == 9:
                    T.dma("sp", kwp_d, stg[:, 0:256])
                    T.dma("sp", vwp_d, stg[:, 256:512])
                else:
                    for s in range(16):
                        T.dma("sp", kws_d[s, 120:128, :], stg[s * 8:(s + 1) * 8, 0:256])
                        T.dma("sp", vws_d[s, 120:128, :], stg[s * 8:(s + 1) * 8, 256:512])
            else:
                T.copy("act" if i % 2 == 0 else "dve", vtm[0:rows, i, :], ps[0:rows, 0:256])
        for sg in range(4):
            pv = PSB(6 + sg % 2)
            for si in range(4):
                s = sg * 4 + si
                for pr in range(2):
                    T.tr(pv[:, (si * 2 + pr) * 128:(si * 2 + pr + 1) * 128], kc_tm[:, s, pr * 128:(pr + 1) * 128], identb)
            T.copy("dve" if sg % 2 == 0 else "act",
                   KTs[:, :, sg * 4:(sg + 1) * 4, 0:128].rearrange("p pr s k -> p s pr k"),
                   pv.rearrange("p (s pr k) -> p s pr k", s=4, pr=2))
        for pr in range(2):
            T.copy("dve", KTs[:, pr, :, 128:136], KT[:, pr, 1154:1282].rearrange("p (s t) -> p s t", s=16))
        for s in range(16):
            T.dma("sp", Vnew[0:8, s, :], vtm[s * 8:(s + 1) * 8, 10, :])

        T.memset("pool", QSpad.rearrange("p a b c -> p (a b c)"), 0.0)
        it = 0
        for hl in range(2):
            slot = wget()
            for jj in range(4):
                qc = hl * 4 + jj
                for ti, (n0, n1) in enumerate(NTQ):
                    W = n1 - n0
                    ps = psb[it % 4]
                    for k in range(16):
                        T.mm(ps[:, 0:W], slot[:, k, jj * 128:(jj + 1) * 128], hT[:, k, n0:n1], start=(k == 0), stop=(k == 15))
                    T.copy("act" if it % 2 == 0 else "dve", QT[:, qc, n0:n1], ps[:, 0:W])
                    if ti == 2:
                        for half in range(2):
                            j = 2 * (qc // 4) + half
                            g = qc % 4
                            T.copy("dve", QSpad[half * 64:(half + 1) * 64, qc // 4, :, 32 * j + 8 * g:32 * j + 8 * g + 8],
                                   ps[half * 64:(half + 1) * 64, 4:132].rearrange("p (s t) -> p s t", s=16))
                    it += 1
        T.dma("pool", Vc, cv_d.rearrange("s k d -> k s d"))
        T.dma("sp", bmini[0:2].rearrange("p a b -> p (a b)"), bmini_d)
        btab_dv = btab_d.rearrange("p (a b) -> p a b", a=16)
        for hg in range(4):
            T.dma("sp", btab[:, hg * 4:(hg + 1) * 4, :], btab_dv[:, hg * 4:(hg + 1) * 4, :])
        T.dma("sp", btabs, btabs_d)

        T.memset("dve", psb[0][:, :], 0.0)
        T.memset("dve", psb[1][:, :], 0.0)
        blocks = []
        blocks.append(dict(rows=2, q0=128, k0=0, nk=130, segs=[(0, 2, vtm[:, 0, :]), (2, 130, vtm[:, 1, :])],
                           bias=lambda e: bmini[0:2, e, :], mask=hmask[0:2, 0:130], mcol=2))
        for n in range(1, 9):
            q0 = 130 + 128 * (n - 1)
            blocks.append(dict(rows=128, q0=q0, k0=q0 - 128, nk=256,
                               segs=[(0, 128, vtm[:, n, :]), (128, 256, vtm[:, n + 1, :])],
                               bias=lambda e: btab[:, e, :], mask=(hmask[:, 0:128] if n == 1 else None), mcol=q0 - 126))
        units = [(bi, e) for bi in range(len(blocks)) for e in range(16)]
        NU = len(units)

        def st_A(u):
            bi, e = units[u]
            b_ = blocks[bi]
            rows, nk = b_["rows"], b_["nk"]
            qc, half = e // 2, e % 2
            pb = half * 64
            T.mm(psb[u % 2][0:rows, 0:nk], QT[pb:pb + 64, qc, b_["q0"]:b_["q0"] + rows],
                 KT[pb:pb + 64, qc // 4, b_["k0"]:b_["k0"] + nk])

        def st_B(u):
            bi, e = units[u]
            b_ = blocks[bi]
            rows, nk = b_["rows"], b_["nk"]
            Ssb = S_sb[u % 2][0:rows, 0:nk + 1]
            T.stt("dve", Ssb, psb[u % 2][0:rows, 0:nk + 1], 0.125, b_["bias"](e), ALU.mult, ALU.add)
            if b_["mask"] is not None:
                mk = b_["mask"].shape[1]
                T.tt("dve", Ssb[:, 0:mk], Ssb[:, 0:mk], b_["mask"], ALU.add)
            T.add("dve", lambda o=ng_all[bi % 2][0:rows, e:e + 1], i=Ssb: nc.vector.tensor_reduce(
                out=o, in_=i, axis=AX.X, op=ALU.max, negate=True), [Ssb], [ng_all[bi % 2][0:rows, e:e + 1]])

        def st_C(u):
            bi, e = units[u]
            b_ = blocks[bi]
            rows, nk = b_["rows"], b_["nk"]
            T.act(P_sb[u % 2][0:rows, 0:nk + 1], S_sb[u % 2][0:rows, 0:nk + 1], AF.Exp, bias=ng_all[bi % 2][0:rows, e:e + 1],
                  accum_out=rs_all[bi % 2][0:rows, e:e + 1])

        def st_E(u):
            bi, e = units[u]
            b_ = blocks[bi]
            rows = b_["rows"]
            PTp = PSB(2 + u % 2)
            for si, (k0, k1, Vap) in enumerate(b_["segs"]):
                T.tr(PTp[0:k1 - k0, si * 128:si * 128 + rows], P_sb[u % 2][0:rows, k0:k1], identb[0:rows, 0:rows])

        def st_F(u):
            bi, e = units[u]
            b_ = blocks[bi]
            rows = b_["rows"]
            PTp = PSB(2 + u % 2)
            if rows == 128 and all(k1 - k0 == 128 for (k0, k1, _) in b_["segs"]):
                ns = len(b_["segs"])
                T.copy("act", PT_sb[u % 2][:, 0:128 * ns], PTp[:, 0:128 * ns])
                return
            for si, (k0, k1, Vap) in enumerate(b_["segs"]):
                T.copy("act", PT_sb[u % 2][0:k1 - k0, si * 128:si * 128 + rows],
                       PTp[0:k1 - k0, si * 128:si * 128 + rows])

        def st_G(u):
            bi, e = units[u]
            b_ = blocks[bi]
            rows = b_["rows"]
            j = 2 * ((e // 2) // 4) + e % 2
            ops_ = psb[4 + 2 * (bi % 2) + e // 8][0:rows, (e % 8) * 64:(e % 8 + 1) * 64]
            ns = len(b_["segs"])
            for si, (k0, k1, Vap) in enumerate(b_["segs"]):
                T.mm(ops_, PT_sb[u % 2][0:k1 - k0, si * 128:si * 128 + rows], Vap[0:k1 - k0, j * 64:(j + 1) * 64],
                     start=(si == 0), stop=(si == ns - 1))

        p1st = {}

        def post1a(bi):
            rows = blocks[bi]["rows"]
            T.recip(rinv[0:rows], rs_all[bi % 2][0:rows])
            A_sb = attn_sb[0:rows]
            for bnk in range(2):
                T.tt("dve", A_sb[:, bnk * 512:(bnk + 1) * 512].rearrange("p (h d) -> p h d", h=8),
                     psb[4 + 2 * (bi % 2) + bnk][0:rows, :].rearrange("p (h d) -> p h d", h=8),
                     bc_mid(rinv[0:rows, bnk * 8:(bnk + 1) * 8], 64), ALU.mult)

        def post1b(bi):
            rows = blocks[bi]["rows"]
            ssq = scol()
            T.act(junk[0:rows, 0:1024], attn_sb[0:rows], AF.Square, accum_out=ssq[0:rows])
            p1st[bi] = ssq

        def post1c(bi):
            rows = blocks[bi]["rows"]
            a = scol()
            T.act(a[0:rows], p1st[bi][0:rows], AF.Sqrt, bias=epsc[0:rows], scale=1.0 / 1024)
            p1st[bi] = a

        def post1d(bi):
            rows = blocks[bi]["rows"]
            b = scol()
            T.recip(b[0:rows], p1st[bi][0:rows])
            T.tt("pool", attn_sc[0:rows], attn_sb[0:rows], b[0:rows, 0:1].to_broadcast([rows, 1024]), ALU.mult)

        def post2(bi):
            b_ = blocks[bi]
            rows, mcol = b_["rows"], b_["mcol"]
            for bnk in range(2):
                pbk = psb[4 + 2 * (bi % 2) + bnk]
                for jj in range(4):
                    c = bnk * 4 + jj
                    T.tr(pbk[:, jj * 128:jj * 128 + rows], attn_sc[0:rows, c * 128:(c + 1) * 128], ident[0:rows, 0:rows])
                T.tt("dve", mixedT[:, bnk * 4:bnk * 4 + 4, mcol:mcol + rows],
                     pbk[:, :].rearrange("p (a b) -> p a b", a=4)[:, :, 0:rows],
                     bc_mid(gOA[:, bnk * 4:bnk * 4 + 4], rows), ALU.mult)

        NSU = 16
        sden = {}

        def ss_A(s):
            for pr in range(2):
                T.mm(psb[s % 2][:, 300:436], QSpad[:, pr, s, :], KTs[:, pr, s, :], start=(pr == 0), stop=(pr == 1))

        def ss_B(s):
            Ssb = S_sb[s % 2][:, 0:137]
            T.stt("dve", Ssb, psb[s % 2][:, 300:437], 0.125, btabs, ALU.mult, ALU.add)
            negm = scol()
            T.add("dve", lambda o=negm, i=Ssb: nc.vector.tensor_reduce(out=o, in_=i, axis=AX.X, op=ALU.max, negate=True),
                  [Ssb], [negm])
            sden[s] = negm

        def ss_C(s):
            negm = sden[s]
            rsum = scol()
            T.act(Pf_sb[s % 2][:, 0:137], S_sb[s % 2][:, 0:137], AF.Exp, bias=negm, accum_out=rsum)
            sden[s] = rsum

        def ss_D(s):
            rv = scol()
            T.recip(rv, sden[s])
            T.ts("dve", P_sb[s % 2][:, 0:136], Pf_sb[s % 2][:, 0:136], rv, None, ALU.mult)

        def ss_E(s):
            PTp = PSB(2 + s % 2)
            Pn = P_sb[s % 2][:, 0:136]
            T.tr(PTp[:, 0:128], Pn[:, 0:128], identb)
            T.tr(PTp[0:8, 128:256], Pn[:, 128:136], identb)

        def ss_F(s):
            PTp = PSB(2 + s % 2)
            T.copy("act", PT_sb[s % 2][:, 0:128], PTp[:, 0:128])
            T.copy("act", PT_sb[s % 2][0:8, 128:256], PTp[0:8, 128:256])

        def ss_G(s):
            Ops = psb[sbank0 + s % 2]
            PTs = PT_sb[s % 2]
            for pr in range(2):
                T.mm(Ops[:, pr * 64:(pr + 1) * 64], Vc[:, s, pr * 128:(pr + 1) * 128], PTs[:, pr * 64:(pr + 1) * 64],
                     start=True, stop=False)
                T.mm(Ops[:, pr * 64:(pr + 1) * 64], Vnew[0:8, s, pr * 128:(pr + 1) * 128],
                     PTs[0:8, 128 + pr * 64:128 + (pr + 1) * 64], start=False, stop=True)

        def ss_H(s):
            Ops = psb[sbank0 + s % 2]
            for half in range(2):
                src = Ops[half * 64:(half + 1) * 64, 0:128].rearrange("p (pr h g t) -> p pr h g t", pr=2, h=2, g=4)[:, :, half, :, :]
                dst = attnT_s[half * 64:(half + 1) * 64, :, 8 * s:8 * s + 8].rearrange("p (pr g) t -> p pr g t", pr=2)
                T.copy("dve", dst, src)

        sbank0 = 4 + 2 * (1 - (len(blocks) - 1) % 2) if blocks else 4

        def sample_iter(t):
            if t == 0 and NSU:
                ss_A(0)
            if t + 1 < NSU:
                ss_A(t + 1)
            if t < NSU:
                ss_B(t)
                ss_C(t)
            if 0 <= t - 1 < NSU:
                ss_D(t - 1)
                ss_E(t - 1)
                ss_F(t - 1)
            if 0 <= t - 2 < NSU:
                ss_G(t - 2)
            if 0 <= t - 3 < NSU:
                ss_H(t - 3)

        pend = {}
        if NU:
            st_A(0)
        for t in range(NU + 10):
            if t + 1 < NU:
                st_A(t + 1)
            if t < NU:
                st_B(t)
                st_C(t)
            if 0 <= t - 1 < NU:
                st_E(t - 1)
                st_F(t - 1)
            if 0 <= t - 2 < NU:
                st_G(t - 2)
                bi, e = units[t - 2]
                if e == 15:
                    pend.setdefault(t, []).append((post1a, bi))
                    pend.setdefault(t + 1, []).append((post1b, bi))
                    pend.setdefault(t + 2, []).append((post1c, bi))
                    pend.setdefault(t + 3, []).append((post1d, bi))
                    pend.setdefault(t + 6, []).append((post2, bi))
            for fn_, bi_ in pend.pop(t, []):
                fn_(bi_)
            if t >= NU:
                sample_iter(t - NU)
        assert not pend
        for t in range(10, NSU + 4):
            sample_iter(t)
        T.act(sq_s, attnT_s.rearrange("p a b -> p (a b)"), AF.Square)
        for c in range(8):
            T.mm(psb[6][:, 0:128], onesf, sq_s[:, c * 128:(c + 1) * 128], start=(c == 0), stop=(c == 7))
        T.act(rb_s, psb[6][:, 0:128], AF.Sqrt, bias=epsc, scale=1.0 / 1024)
        T.recip(rb_s, rb_s)
        for c in range(8):
            T.stt("dve", mixedT[:, c, 1028:1156], attnT_s[:, c, :], gOA[:, c:c + 1], rb_s, ALU.mult, ALU.mult)

        rt = [(130 + 128 * i, 258 + 128 * i) for i in range(8)] + [(1154, 1282)]
        dgb = [V(O_B + 37440 + i * 512, [128]) for i in range(2)]
        rbcb = [V(O_B + 37440 + 1024 + i * 512, [128]) for i in range(2)]
        dummy6 = V(O_B + 37440 + 2048, [512], BF16)
        n2 = {"g": 0, "e": 0, "slot": 0}
        n2q = []
        evt = [V(O_B + 37440 + 3072 + i * 512, [128]) for i in range(4)]

        n2rs = {}

        def norm2_stats(r):
            ssq = scol()
            T.reduce("dve", ssq, ssq2[:, 4 * r:4 * r + 4], ALU.add)
            rs = rstd_of(ssq, 128, 1.0 / D)
            T.ts("dve", dgb[r % 2], ident, rs, None, ALU.mult)

        def norm2_grp(r, kb):
            c0, c1 = rt[r]
            rb = rbcb[r % 2]
            if kb == 0:
                T.mm(psb[6][:, 0:128], onesf, dgb[r % 2])
                T.copy("act", rb, psb[6][:, 0:128])
            bank = 3 + n2["g"] % 3
            n2["g"] += 1
            for j in range(4):
                k = kb * 4 + j
                T.tr(psb[bank][:, j * 128:(j + 1) * 128], x2acc[:, r, k * 128:(k + 1) * 128], ident)
            for j in range(4):
                k = kb * 4 + j
                if kb % 2 == 0:
                    T.stt("dve", h2T[:, k, c0 - 128:c1 - 128], psb[bank][:, j * 128:(j + 1) * 128], gF[:, k:k + 1], rb,
                          ALU.mult, ALU.mult)
                else:
                    tmp = evt[n2["e"] % 4]
                    n2["e"] += 1
                    T.act(tmp, psb[bank][:, j * 128:(j + 1) * 128], AF.Copy, scale=gF[:, k:k + 1])
                    T.tt("pool", h2T[:, k, c0 - 128:c1 - 128], tmp, rb, ALU.mult)

        it = 0
        for nb in range(4):
            slot = wget()
            for jj in range(4):
                oc = nb * 4 + jj
                for k in range(16):
                    T.mm(psb[7][:, oc * 2:oc * 2 + 2], slot[:, k, jj * 128:(jj + 1) * 128], mixedT[:, k, 2:4],
                         start=(k == 0), stop=(k == 15))
            for r, (c0, c1) in enumerate(rt):
                ps = psb[it % 3]
                xr = xres[it % 2]
                T.dma("sp", xr, xin[c0:c1, nb * 512:(nb + 1) * 512])
                for k in range(16):
                    T.mm(ps[:, :], mixedT[:, k, c0 - 126:c1 - 126], slot[:, k, :], start=(k == 0), stop=(k == 15))
                    if nb == 3 and k % 4 == 3:
                        n2["slot"] += 1
                        if n2q and n2q[0][2] <= n2["slot"]:
                            g_ = n2q.pop(0)
                            norm2_grp(g_[0], g_[1])
                T.tt("dve", x2acc[:, r, nb * 512:(nb + 1) * 512], ps[:, :], xr, ALU.add)
                T.act(dummy6, x2acc[:, r, nb * 512:(nb + 1) * 512], AF.Square, accum_out=ssq2[:, 4 * r + nb:4 * r + nb + 1])
                if nb == 3:
                    norm2_stats(r)
                    for kb in range(4):
                        n2q.append((r, kb, n2["slot"] + 3))
                it += 1
        for g_ in n2q:
            norm2_grp(g_[0], g_[1])
        T.tt("dve", x2Tm, psb[7][:, 0:32], xTm, ALU.add)
        T.act(sqm, x2Tm, AF.Square)
        for k in range(16):
            T.mm(psb[6][:, 0:2], onesf, sqm[:, 2 * k:2 * k + 2], start=(k == 0), stop=(k == 15))
        T.act(rm, psb[6][:, 0:2], AF.Sqrt, bias=epsc, scale=1.0 / D)
        T.recip(rm, rm)
        x2Tm3 = x2Tm.rearrange("p (k t) -> p k t", k=16)
        T.tt("dve", x2Tm3, x2Tm3, rm.unsqueeze(1).to_broadcast([128, 16, 2]), ALU.mult)
        T.tt("dve", h2T[:, :, 0:2], x2Tm3, bc_mid(gF, 2), ALU.mult)

        sfst = [V(O_C + piece * 5632, [1408]) for piece in range(4)]
        for piece in range(4):
            T.dma("sp", sfst[piece][0:32, :], sf_d[:, piece * 1408:(piece + 1) * 1408])
        for piece in range(4):
            sfstage = sfst[piece]
            pbk = psb[4 + piece % 2]
            for cc in range(11):
                T.tr(pbk[:, cc * 32:(cc + 1) * 32], sfstage[0:32, cc * 128:(cc + 1) * 128], ident[0:32, 0:32])
            T.copy("dve", sfT[:, piece * 11:(piece + 1) * 11, :].rearrange("p a b -> p (a b)"), pbk[:, 0:352])
        NTF = [(128, 640), (640, 1152), (1152, 1282)]
        gstate = {"it": 0, "dit": 0}

        def gateup(grp):
            for hg in range(2):
                gslot = wget()
                uslot = wget(second=True)
                for ci in range(2):
                    f = grp * 4 + hg * 2 + ci
                    fi = hg * 2 + ci
                    ab = actb[grp % 2]
                    for ti, (n0, n1) in enumerate(NTF):
                        it = gstate["it"]
                        gstate["it"] += 1
                        W = n1 - n0
                        s0 = (it % 2) * 2
                        pG, pU = psb[s0], psb[s0 + 1]
                        for k in range(16):
                            T.mm(pG[:, 0:W], gslot[:, k, ci * 128:(ci + 1) * 128], h2T[:, k, n0 - 128:n1 - 128],
                                 start=(k == 0), stop=(k == 15))
                        for k in range(16):
                            T.mm(pU[:, 0:W], uslot[:, k, ci * 128:(ci + 1) * 128], h2T[:, k, n0 - 128:n1 - 128],
                                 start=(k == 0), stop=(k == 15))
                        T.copy("act", g_sb[:, n0:n1], pG[:, 0:W])
                        lo, hi = max(n0, 130), min(n1, 1154)
                        if hi > lo:
                            L = hi - lo
                            tA = tAb[it % 2][:, 0:L]
                            T.act(tA, g_sb[:, lo - 2:hi - 2], AF.Copy, scale=fw(0, f))
                            T.stt("dve", tA, g_sb[:, lo - 1:hi - 1], fw(1, f), tA, ALU.mult, ALU.add)
                            T.stt("dve", tA, g_sb[:, lo:hi], fw(2, f), tA, ALU.mult, ALU.add)
                            tS = tSb[:, 0:L]
                            T.act(tS, tA, AF.Silu, bias=fb(f))
                            T.tt("dve", ab[:, fi, lo - 130:hi - 130], pU[:, lo - n0:hi - n0], tS, ALU.mult)
                        if ti == 2:
                            T.copy("dve", gs_ext[:, :, 2:10], g_sb[:, 1154:1282].rearrange("p (s t) -> p s t", s=16))
                            T.copy("act", gs_ext[:, :, 0:2], sfT[:, f, :].rearrange("p (s r) -> p s r", s=16))
                            tA3 = tAb[(it + 1) % 2][:, 0:128].rearrange("p (s t) -> p s t", s=16)
                            T.act(tA3, gs_ext[:, :, 0:8], AF.Copy, scale=fw(0, f))
                            T.stt("dve", tA3, gs_ext[:, :, 1:9], fw(1, f), tA3, ALU.mult, ALU.add)
                            T.stt("dve", tA3, gs_ext[:, :, 2:10], fw(2, f), tA3, ALU.mult, ALU.add)
                            tS3 = tSb[:, 0:128].rearrange("p (s t) -> p s t", s=16)
                            T.act(tS3, tA3, AF.Silu, bias=fb(f))
                            T.tt("dve", ab[:, fi, 1024:1152].rearrange("p (s t) -> p s t", s=16),
                                 pU[:, 2:130].rearrange("p (s t) -> p s t", s=16), tS3, ALU.mult)
                            T.copy("act", ffn_keep[:, f, 0:2], g_sb[:, 1152:1154])
                            T.copy("act", ffn_keep[:, f, 2:34].rearrange("p (s r) -> p s r", s=16), gs_ext[:, :, 8:10])

        gfin_bc = V(O_B, [2048])
        dummy8 = V(O_B + 8192, [512], BF16)

        def final_tile(r):
            x2 = x2acc[:, r, :]
            ssq = scol()
            T.reduce("dve", ssq, ssq2[:, 4 * r:4 * r + 4], ALU.add)
            rs = rstd_of(ssq, 128, 1.0 / D)
            T.stt("dve", x2[:, 0:1024], x2[:, 0:1024], rs, gfin_bc[:, 0:1024], ALU.mult, ALU.mult)
            T.act(x2[:, 1024:2048], x2[:, 1024:2048], AF.Copy, scale=rs)
            T.tt("pool", x2[:, 1024:2048], x2[:, 1024:2048], gfin_bc[:, 1024:2048], ALU.mult)
            if r < 8:
                T.dma("sp", yp_d[r * 128:(r + 1) * 128, :], x2)
            else:
                T.dma("sp", ys_d, x2)

        def down(grp, last=False):
            dsl = [wget(), wget(second=True)]
            ab = actb[grp % 2]
            if last:
                T.dma("sp", gfin_bc, gfin_d.partition_broadcast(128))
            for r in range(9):
                for nb in range(4):
                    ps = psb[4 + gstate["dit"] % 4]
                    gstate["dit"] += 1
                    for fi in range(4):
                        T.mm(ps[:, :], ab[:, fi, r * 128:(r + 1) * 128], dsl[fi // 2][:, fi % 2, nb * 512:(nb + 1) * 512],
                             start=(fi == 0), stop=(fi == 3))
                    T.tt("dve", x2acc[:, r, nb * 512:(nb + 1) * 512], ps[:, :], x2acc[:, r, nb * 512:(nb + 1) * 512], ALU.add)
                    if last:
                        T.act(dummy8, x2acc[:, r, nb * 512:(nb + 1) * 512], AF.Square,
                              accum_out=ssq2[:, 4 * r + nb:4 * r + nb + 1])
                if last and r >= 1:
                    final_tile(r - 1)
            if last:
                final_tile(8)

        for grp in range(11):
            gateup(grp)
            if grp >= 1:
                down(grp - 1)
        for rnd in range(3):
            chunks = list(range(rnd * 16, min(NFF, rnd * 16 + 16)))
            for cc, f in enumerate(chunks):
                T.tr(psb[cc // 4][0:34, (cc % 4) * 128:(cc % 4 + 1) * 128], ffn_keep[:, f, :], ident)
            nbk = (len(chunks) + 3) // 4
            for b in range(nbk):
                T.copy("act" if b % 2 == 0 else "dve", ffstage[0:34, b * 512:(b + 1) * 512], psb[b][0:34, :])
            ncols = len(chunks) * 128
            T.dma("sp", ffp_d[:, rnd * 2048:rnd * 2048 + ncols], ffstage[0:2, 0:ncols])
            T.dma("sp", ffs_d[:, rnd * 2048:rnd * 2048 + ncols], ffstage[2:34, 0:ncols])

        down(10, last=True)
        assert wstate["cons"] == len(wspecs), (wstate, len(wspecs))
        T.emit()
    return nc


def _const_tables(sinks):
    slopes = np.exp2(-8.0 * np.arange(1, 17, dtype=np.float32) / 16.0).astype(np.float32)
    q = np.arange(128)[:, None]
    k = np.arange(256)[None, :]
    dist = (q - k + 128).astype(np.float32)
    valid = (dist >= 0) & (dist <= 128)
    btab = np.empty((128, 16, 257), np.float32)
    for e, h in enumerate(PERM_HEADS):
        btab[:, e, 0:256] = np.where(valid, -slopes[h] * dist, NEG)
        btab[:, e, 256] = sinks[h]
    r = np.arange(128)
    t = (r % 8)[:, None]
    kj = np.arange(136)[None, :]
    dist_s = (t + 128 - kj).astype(np.float32)
    valid_s = (dist_s >= 0) & (dist_s <= 128)
    btabs = np.empty((128, 137), np.float32)
    btabs[:, 0:136] = np.where(valid_s, -slopes[r // 8][:, None] * dist_s, NEG)
    btabs[:, 136] = sinks[r // 8]
    i = np.arange(2)[:, None]
    kk = np.arange(130)[None, :]
    dist_m = (128 + i - kk).astype(np.float32)
    valid_m = (dist_m >= 0) & (dist_m <= 128)
    bmini = np.empty((2, 16, 131), np.float32)
    for e, h in enumerate(PERM_HEADS):
        bmini[:, e, 0:130] = np.where(valid_m, -slopes[h] * dist_m, NEG)
        bmini[:, e, 130] = sinks[h]
    return btab.reshape(128, 16 * 257), btabs, bmini.reshape(2, 16 * 131)


_NC_CACHE = {}


def kernel(x_prompt, x_sample, cache_k_window, cache_v_window, state_conv, state_ffn_conv,
           g_attn_norm, w_in, attn_sinks, conv_w, g_out_attn, g_out_conv, w_out,
           g_ffn_norm, w_gate, w_up, ffn_conv_w, ffn_conv_b, w_down, g_final):
    f32 = np.float32
    x_prompt = np.asarray(x_prompt, f32)
    x_sample = np.asarray(x_sample, f32)
    featperm = np.concatenate([np.arange(h * 64, (h + 1) * 64) for h in PERM_HEADS])
    w_in0 = np.asarray(w_in, f32)[0]
    cols = []
    for c in range(8):
        for base in (1536, 2560, 3584):
            cols.append(np.arange(base + c * 128, base + (c + 1) * 128))
    cols.append(np.arange(1024, 1536))
    cols.append(featperm)
    win = np.ascontiguousarray(w_in0[:, np.concatenate(cols)])
    w_out0 = np.asarray(w_out, f32)[0]
    wout = np.ascontiguousarray(np.concatenate([w_out0[featperm], w_out0[1024:]], axis=0))
    wg = np.ascontiguousarray(np.asarray(w_gate, f32)[0])
    wu = np.ascontiguousarray(np.asarray(w_up, f32)[0])
    wd = np.ascontiguousarray(np.asarray(w_down, f32)[0])

    def fm(v, n):
        return np.asarray(v, f32).reshape(n, 128).T

    vecs = np.concatenate([
        fm(np.asarray(g_attn_norm)[0], 16), fm(np.asarray(g_ffn_norm)[0], 16),
        fm(np.asarray(g_out_attn, f32)[0][featperm], 8), fm(np.asarray(g_out_conv)[0], 8),
        np.concatenate([fm(np.asarray(conv_w)[0, j], 8) for j in range(3)], axis=1),
        np.concatenate([fm(np.asarray(ffn_conv_w)[0, j], NFF) for j in range(3)], axis=1),
        fm(np.asarray(ffn_conv_b)[0], NFF)], axis=1).astype(f32)
    assert vecs.shape == (128, V_N)
    vecs = np.ascontiguousarray(vecs)
    sinks = np.asarray(attn_sinks, f32)[0]
    sinkbc = np.ascontiguousarray(np.broadcast_to(sinks[PERM_HEADS][None, :], (128, 16)))
    sinkrows = np.ascontiguousarray(np.repeat(sinks, 8)[:, None])
    btab, btabs, bmini = _const_tables(sinks)
    idn = np.eye(128, dtype=f32)
    gfin = np.ascontiguousarray(np.asarray(g_final, f32))
    ck_all = np.asarray(cache_k_window, f32)[0].reshape(128, 128, 256)
    cv_all = np.asarray(cache_v_window, f32)[0].reshape(128, 128, 256)
    sc_all = np.asarray(state_conv, f32)[0]
    sf_all = np.asarray(state_ffn_conv, f32)[0]

    in_maps = []
    for c in range(8):
        b, half = c // 2, c % 2
        xin = np.zeros((NT, D), f32)
        if half == 1:
            xin[0:130] = x_prompt[b, 1024 - 130:1024]
        xin[130:1154] = x_prompt[b, half * 1024:(half + 1) * 1024]
        xin[1154:1282] = x_sample[16 * c:16 * (c + 1)].reshape(128, D)
        hm = np.full((128, 130), NEG if half == 0 else 0.0, f32)
        in_maps.append({
            "xin": xin,
            "ck": np.ascontiguousarray(ck_all[16 * c:16 * (c + 1)]),
            "cv": np.ascontiguousarray(cv_all[16 * c:16 * (c + 1)]),
            "sc": np.ascontiguousarray(sc_all[16 * c:16 * (c + 1)].reshape(32, 1024)),
            "sf": np.ascontiguousarray(sf_all[16 * c:16 * (c + 1)].reshape(32, DFF)),
            "win": win, "wout": wout, "wg": wg, "wu": wu, "wd": wd,
            "vecs": vecs, "sinkbc": sinkbc, "sinkrows": sinkrows,
            "btab": btab, "btabs": btabs, "bmini": bmini, "hmask": hm, "idn": idn, "gfin": gfin,
        })

    if "nc" not in _NC_CACHE:
        _NC_CACHE["nc"] = build_nc()
    nc = _NC_CACHE["nc"]
    res = run_bass_kernel_spmd(nc, in_maps, core_ids=list(range(8)))
    R = res.results

    y_prompt = np.empty((4, 2048, D), f32)
    y_sample = np.empty((128, 8, D), f32)
    kwp = np.empty((1, 4, 128, 4, 64), f32)
    vwp = np.empty((1, 4, 128, 4, 64), f32)
    cvp = np.empty((1, 4, 2, 1024), f32)
    ffp = np.empty((1, 4, 2, DFF), f32)
    kws = np.empty((1, 128, 128, 4, 64), f32)
    vws = np.empty((1, 128, 128, 4, 64), f32)
    cvs = np.empty((1, 128, 2, 1024), f32)
    ffs = np.empty((1, 128, 2, DFF), f32)
    for c in range(8):
        b, half = c // 2, c % 2
        r = R[c]
        y_prompt[b, half * 1024:(half + 1) * 1024] = r["yp"]
        y_sample[16 * c:16 * (c + 1)] = r["ys"].reshape(16, 8, D)
        if half == 1:
            kwp[0, b] = r["kwp"].reshape(128, 4, 64)
            vwp[0, b] = r["vwp"].reshape(128, 4, 64)
            cvp[0, b] = r["cvp"]
            ffp[0, b] = r["ffp"]
        kws[0, 16 * c:16 * (c + 1)] = r["kws"].reshape(16, 128, 4, 64)
        vws[0, 16 * c:16 * (c + 1)] = r["vws"].reshape(16, 128, 4, 64)
        cvs[0, 16 * c:16 * (c + 1)] = r["cvs"].reshape(16, 2, 1024)
        ffs[0, 16 * c:16 * (c + 1)] = r["ffs"].reshape(16, 2, DFF)
    return (y_prompt, y_sample, kwp, vwp, cvp, ffp, kws, vws, cvs, ffs)
```
